# Optimizing a Trainium2 kernel written in Bass

```python
import numpy as np
import jax
import jax.numpy as jnp
from jax import lax

D_MODEL = 2048
BATCH = 8
SEQ = 2048
DEPTH = 2

HEAD_DIM = 128
MIX_HEADS = D_MODEL // HEAD_DIM
MLA_HEADS = MIX_HEADS // 2
NSA_HEADS = MIX_HEADS // 4
MOBA_HEADS = MIX_HEADS - MLA_HEADS - NSA_HEADS
MIX_WIDTH = (MLA_HEADS + NSA_HEADS + MOBA_HEADS) * HEAD_DIM
ROPE_THETA = 500000.0
PARTIAL_ROT = HEAD_DIM // 4
QBLK = 128
MLA_Q_RANK = 512
MLA_KV_RANK = 512
MLA_NOPE = 128
MLA_ROPE = 64
MLA_V = HEAD_DIM
NSA_CMP_LEN = 32
NSA_CMP_STRIDE = 16
NSA_CMP_HIDDEN = HEAD_DIM
NSA_SEL_LEN = 64
NSA_SEL_TOPK = 16
NSA_WINDOW = 512
NSA_FORCE_SCORE = 1.0e4
MOBA_BLOCK = 256
MOBA_TOPK = 3
MOBA_QCHUNK = 64
MEM_LEN = 256
MEM_HEADS = 4
D_FF = 4 * D_MODEL
DEEPNORM_ALPHA = (2 * DEPTH) ** 0.25
DEEPNORM_BETA = (8 * DEPTH) ** -0.25
IN_SPLITS = (MLA_Q_RANK, MLA_KV_RANK, MLA_ROPE,
             NSA_HEADS * HEAD_DIM, HEAD_DIM, HEAD_DIM, HEAD_DIM, HEAD_DIM, HEAD_DIM, HEAD_DIM, 3 * NSA_HEADS,
             MOBA_HEADS * HEAD_DIM, MOBA_HEADS * HEAD_DIM, MOBA_HEADS * HEAD_DIM)
IN_WIDTH = sum(IN_SPLITS)

kernel_name = 'hybrid_mla_nsa_moba_deepnorm'


def layer_norm(x, g, b, eps=1e-5):
    xf = x.astype(jnp.float32)
    mu = jnp.mean(xf, axis=-1, keepdims=True)
    var = jnp.mean(jnp.square(xf - mu), axis=-1, keepdims=True)
    y = (xf - mu) * lax.rsqrt(var + eps)
    return (y * g.astype(jnp.float32) + b.astype(jnp.float32)).astype(x.dtype)


def rms_norm(x, g, eps=1e-6):
    xf = x.astype(jnp.float32)
    y = xf * lax.rsqrt(jnp.mean(jnp.square(xf), axis=-1, keepdims=True) + eps)
    return (y * g.astype(jnp.float32)).astype(x.dtype)


def masked_softmax(s, mask):
    s = jnp.where(mask, s.astype(jnp.float32), -jnp.inf)
    m = jnp.max(s, axis=-1, keepdims=True)
    m = jnp.where(jnp.isfinite(m), m, 0.0)
    e = jnp.exp(s - m)
    return e / jnp.maximum(jnp.sum(e, axis=-1, keepdims=True), 1e-30)


def rope_tables(n_pos, dim):
    inv = ROPE_THETA ** (-jnp.arange(0, dim, 2, dtype=jnp.float32) / dim)
    ang = jnp.arange(n_pos, dtype=jnp.float32)[:, None] * inv[None, :]
    return jnp.cos(ang), jnp.sin(ang)


def apply_rope(x, cos, sin):
    half = x.shape[-1] // 2
    c = cos[None, :, None, :].astype(x.dtype)
    s = sin[None, :, None, :].astype(x.dtype)
    x1, x2 = x[..., :half], x[..., half:]
    return jnp.concatenate([x1 * c - x2 * s, x2 * c + x1 * s], axis=-1)


def partial_rope(x, cos, sin):
    rot = 2 * cos.shape[-1]
    return jnp.concatenate([apply_rope(x[..., :rot], cos, sin), x[..., rot:]], axis=-1)


def map_batch_chunks(fn, n_batch, n_chunk):
    items = jnp.arange(n_batch * n_chunk)
    return lax.map(lambda i: fn(i // n_chunk, i % n_chunk), items)


def causal_attention_qblocks(q, k, v, scale):
    B, S, H, dk = q.shape
    nq = S // QBLK
    qb = q.reshape(B, nq, QBLK, H, dk).transpose(1, 0, 2, 3, 4)
    kpos = jnp.arange(S)

    def one(args):
        qi, i = args
        s = jnp.einsum('bqhd,bkhd->bhqk', qi, k).astype(jnp.float32) * scale
        qpos = i * QBLK + jnp.arange(QBLK)
        p = masked_softmax(s, kpos[None, :] <= qpos[:, None])
        return jnp.einsum('bhqk,bkhd->bqhd', p.astype(v.dtype), v)

    out = lax.map(one, (qb, jnp.arange(nq)))
    return out.transpose(1, 0, 2, 3, 4).reshape(B, S, H, v.shape[-1])


def mla_mixer(c_q, c_kv, k_rope_in, q_norm, kv_norm, w_uq, w_ukv):
    B, S, _ = c_q.shape
    cos, sin = rope_tables(S, MLA_ROPE)
    q = (rms_norm(c_q, q_norm) @ w_uq).reshape(B, S, MLA_HEADS, MLA_NOPE + MLA_ROPE)
    q = jnp.concatenate([q[..., :MLA_NOPE], apply_rope(q[..., MLA_NOPE:], cos, sin)], axis=-1)
    kv = (rms_norm(c_kv, kv_norm) @ w_ukv).reshape(B, S, MLA_HEADS, MLA_NOPE + MLA_V)
    k_rope = apply_rope(k_rope_in[:, :, None, :], cos, sin)
    k = jnp.concatenate([kv[..., :MLA_NOPE], jnp.broadcast_to(k_rope, (B, S, MLA_HEADS, MLA_ROPE))], axis=-1)
    v = kv[..., MLA_NOPE:]
    o = causal_attention_qblocks(q, k, v, (MLA_NOPE + MLA_ROPE) ** -0.5)
    return o.reshape(B, S, MLA_HEADS * MLA_V)


def nsa_mixer(q_in, k_c, v_c, k_s, v_s, k_w, v_w, gate_in, cmp_w1, cmp_w2, cmp_pos):
    B, S, _ = q_in.shape
    H, Dh = NSA_HEADS, HEAD_DIM
    cos, sin = rope_tables(S, PARTIAL_ROT)
    q = partial_rope(q_in.reshape(B, S, H, Dh), cos, sin)
    rot = lambda t: partial_rope(t[:, :, None, :], cos, sin)[:, :, 0, :]
    k_c, k_s, k_w = rot(k_c), rot(k_s), rot(k_w)
    scale = Dh ** -0.5
    tpos = jnp.arange(S)

    n_cmp = (S - NSA_CMP_LEN) // NSA_CMP_STRIDE + 1
    starts = np.arange(n_cmp) * NSA_CMP_STRIDE
    gidx = starts[:, None] + np.arange(NSA_CMP_LEN)[None, :]

    def compress(t, i):
        blk = t[:, gidx] + cmp_pos[i]
        hid = jax.nn.gelu(blk.reshape(B, n_cmp, NSA_CMP_LEN * Dh) @ cmp_w1[i])
        return hid @ cmp_w2[i]

    kc, vc = compress(k_c, 0), compress(v_c, 1)
    s_cmp = jnp.einsum('bshd,bnd->bhsn', q, kc).astype(jnp.float32) * scale
    cmp_mask = (starts + NSA_CMP_LEN - 1)[None, :] <= tpos[:, None]
    p_cmp = masked_softmax(s_cmp, cmp_mask)
    o_cmp = jnp.einsum('bhsn,bnd->bshd', p_cmp.astype(vc.dtype), vc)

    n_sel = S // NSA_SEL_LEN
    sel_start = np.arange(n_sel) * NSA_SEL_LEN
    overlap = ((starts[:, None] < sel_start[None, :] + NSA_SEL_LEN)
               & (starts[:, None] + NSA_CMP_LEN > sel_start[None, :])).astype(np.float32)
    imp = jnp.einsum('bhsn,nj->bsj', p_cmp, jnp.asarray(overlap))
    cur = tpos // NSA_SEL_LEN
    j = jnp.arange(n_sel)
    eligible = j[None, :] <= cur[:, None]
    forced = (j[None, :] == 0) | (j[None, :] == cur[:, None]) | (j[None, :] == cur[:, None] - 1)
    score = jnp.where(eligible, jnp.where(forced, NSA_FORCE_SCORE, imp), -jnp.inf)
    _, sel_idx = lax.top_k(score, min(NSA_SEL_TOPK, n_sel))
    n_k = sel_idx.shape[-1]
    n_ch = S // QBLK

    def sel_chunk(b, c):
        t0 = c * QBLK
        qc = lax.dynamic_slice_in_dim(q[b], t0, QBLK, 0)
        ic = lax.dynamic_slice_in_dim(sel_idx[b], t0, QBLK, 0)
        kg = k_s[b].reshape(n_sel, NSA_SEL_LEN, Dh)[ic].reshape(QBLK, n_k * NSA_SEL_LEN, Dh)
        vg = v_s[b].reshape(n_sel, NSA_SEL_LEN, Dh)[ic].reshape(QBLK, n_k * NSA_SEL_LEN, Dh)
        kpos = (ic[:, :, None] * NSA_SEL_LEN + jnp.arange(NSA_SEL_LEN)).reshape(QBLK, -1)
        mask = kpos <= (t0 + jnp.arange(QBLK))[:, None]
        s = jnp.einsum('thd,tkd->htk', qc, kg).astype(jnp.float32) * scale
        p = masked_softmax(s, mask[None])
        return jnp.einsum('htk,tkd->thd', p.astype(vg.dtype), vg)

    o_sel = map_batch_chunks(sel_chunk, B, n_ch).reshape(B, S, H, Dh)

    n_band = NSA_WINDOW // QBLK + 1
    bidx = np.arange(n_ch)[:, None] + np.arange(n_band)[None, :]

    def band(t):
        tp = jnp.pad(t, ((0, 0), (NSA_WINDOW, 0), (0, 0))).reshape(B, n_ch + NSA_WINDOW // QBLK, QBLK, Dh)
        return tp[:, bidx].reshape(B, n_ch, n_band * QBLK, Dh)

    kb, vb = band(k_w), band(v_w)
    kpos = (np.arange(n_ch)[:, None] * QBLK - NSA_WINDOW) + np.arange(n_band * QBLK)[None, :]
    qpos = np.arange(n_ch * QBLK).reshape(n_ch, QBLK)
    diff = qpos[:, :, None] - kpos[:, None, :]
    wmask = jnp.asarray((diff >= 0) & (diff < NSA_WINDOW) & (kpos[:, None, :] >= 0))
    qb = q.reshape(B, n_ch, QBLK, H, Dh)
    s_w = jnp.einsum('bnqhd,bnkd->bnhqk', qb, kb).astype(jnp.float32) * scale
    p_w = masked_softmax(s_w, wmask[None, :, None])
    o_win = jnp.einsum('bnhqk,bnkd->bnqhd', p_w.astype(vb.dtype), vb).reshape(B, S, H, Dh)

    g = jax.nn.sigmoid(gate_in.astype(jnp.float32)).reshape(B, S, H, 3).astype(q.dtype)
    o = g[..., 0:1] * o_cmp + g[..., 1:2] * o_sel + g[..., 2:3] * o_win
    return o.reshape(B, S, H * Dh)


def moba_mixer(q_in, k_in, v_in):
    B, S, _ = q_in.shape
    H, Dh, BLK = MOBA_HEADS, HEAD_DIM, MOBA_BLOCK
    cos, sin = rope_tables(S, PARTIAL_ROT)
    q = partial_rope(q_in.reshape(B, S, H, Dh), cos, sin)
    k = partial_rope(k_in.reshape(B, S, H, Dh), cos, sin)
    v = v_in.reshape(B, S, H, Dh)
    Sp = -(-S // BLK) * BLK
    padw = ((0, 0), (0, Sp - S), (0, 0), (0, 0))
    q, k, v = jnp.pad(q, padw), jnp.pad(k, padw), jnp.pad(v, padw)
    nb = Sp // BLK
    kb_all = k.reshape(B, nb, BLK, H, Dh).transpose(0, 3, 1, 2, 4)
    vb_all = v.reshape(B, nb, BLK, H, Dh).transpose(0, 3, 1, 2, 4)
    kmean = jnp.mean(kb_all.astype(jnp.float32), axis=3)
    gate = jnp.einsum('bshd,bhnd->bhsn', q.astype(jnp.float32), kmean)
    tpos = jnp.arange(Sp)
    cur = tpos // BLK
    gate = jnp.where(jnp.arange(nb)[None, :] < cur[:, None], gate, -jnp.inf)
    n_k = min(MOBA_TOPK, nb)
    _, sel = lax.top_k(gate, n_k)
    valid = sel < cur[None, None, :, None]
    scale = Dh ** -0.5
    n_ch = Sp // MOBA_QCHUNK
    gather = jax.vmap(lambda blocks, idx: blocks[idx])

    def chunk(b, c):
        t0 = c * MOBA_QCHUNK
        blk = t0 // BLK
        qc = lax.dynamic_slice_in_dim(q[b], t0, MOBA_QCHUNK, 0)
        ic = lax.dynamic_slice_in_dim(sel[b], t0, MOBA_QCHUNK, 1)
        vl = lax.dynamic_slice_in_dim(valid[b], t0, MOBA_QCHUNK, 1)
        kg = gather(kb_all[b], ic).reshape(H, MOBA_QCHUNK, n_k * BLK, Dh)
        vg = gather(vb_all[b], ic).reshape(H, MOBA_QCHUNK, n_k * BLK, Dh)
        ko = lax.dynamic_slice_in_dim(kb_all[b], blk, 1, 1)[:, 0]
        vo = lax.dynamic_slice_in_dim(vb_all[b], blk, 1, 1)[:, 0]
        tq = t0 + jnp.arange(MOBA_QCHUNK)
        own_mask = (blk * BLK + jnp.arange(BLK))[None, :] <= tq[:, None]
        sel_mask = jnp.repeat(vl, BLK, axis=-1)
        s = jnp.concatenate([jnp.einsum('thd,htkd->htk', qc, kg),
                             jnp.einsum('thd,hkd->htk', qc, ko)], axis=-1).astype(jnp.float32) * scale
        mask = jnp.concatenate([sel_mask, jnp.broadcast_to(own_mask[None], (H, MOBA_QCHUNK, BLK))], axis=-1)
        p = masked_softmax(s, mask).astype(v.dtype)
        n = n_k * BLK
        return jnp.einsum('htk,htkd->thd', p[..., :n], vg) + jnp.einsum('htk,hkd->thd', p[..., n:], vo)

    o = map_batch_chunks(chunk, B, n_ch).reshape(B, Sp, H, Dh)[:, :S]
    return o.reshape(B, S, H * Dh)


def hybrid_mixer(h, w_in, mla_q_norm, mla_kv_norm, mla_w_uq, mla_w_ukv, nsa_cmp_w1, nsa_cmp_w2, nsa_cmp_pos, w_out):
    proj = h @ w_in
    offs = np.cumsum(IN_SPLITS)[:-1].tolist()
    (c_q, c_kv, k_rope, nq, nkc, nvc, nks, nvs, nkw, nvw, ng, mq, mk, mv) = jnp.split(proj, offs, axis=-1)
    o_a = mla_mixer(c_q, c_kv, k_rope, mla_q_norm, mla_kv_norm, mla_w_uq, mla_w_ukv)
    o_b = nsa_mixer(nq, nkc, nvc, nks, nvs, nkw, nvw, ng, nsa_cmp_w1, nsa_cmp_w2, nsa_cmp_pos)
    o_c = moba_mixer(mq, mk, mv)
    return jnp.concatenate([o_a, o_b, o_c], axis=-1) @ w_out


def memory_cross_attention(h, mem, wq, wkv, wo):
    B, S, D = h.shape
    M = mem.shape[1]
    hd = D // MEM_HEADS
    q = (h @ wq).reshape(B, S, MEM_HEADS, hd)
    kv = (mem @ wkv).reshape(B, M, 2, MEM_HEADS, hd)
    s = jnp.einsum('bshd,bmhd->bhsm', q, kv[:, :, 0]).astype(jnp.float32) * hd ** -0.5
    p = jax.nn.softmax(s, axis=-1).astype(h.dtype)
    return jnp.einsum('bhsm,bmhd->bshd', p, kv[:, :, 1]).reshape(B, S, D) @ wo


def squared_relu_mlp(h, w1, w2):
    return jnp.square(jax.nn.relu(h @ w1)) @ w2


def setup_inputs(seed: int = 0) -> dict:
    key = jax.random.key(seed)
    ks = jax.random.split(key, 24)
    L, D = DEPTH, D_MODEL
    f32 = jnp.float32

    def w(k, shape, fan_in, scale=1.0):
        return jax.random.normal(k, shape, f32) * (scale * fan_in ** -0.5)

    def gain(k, shape):
        return 1.0 + 0.02 * jax.random.normal(k, shape, f32)

    def bias(k, shape):
        return 0.02 * jax.random.normal(k, shape, f32)

    return {
        'x': jax.random.normal(ks[0], (BATCH, SEQ, D), f32),
        'mem': jax.random.normal(ks[1], (BATCH, MEM_LEN, D), f32),
        'ln_in_g': gain(ks[2], (D,)),
        'ln_in_b': bias(ks[3], (D,)),
        'w_in': w(ks[4], (L, D, IN_WIDTH), D),
        'mla_q_norm': gain(ks[5], (L, MLA_Q_RANK)),
        'mla_kv_norm': gain(ks[6], (L, MLA_KV_RANK)),
        'mla_w_uq': w(ks[7], (L, MLA_Q_RANK, MLA_HEADS * (MLA_NOPE + MLA_ROPE)), MLA_Q_RANK),
        'mla_w_ukv': w(ks[8], (L, MLA_KV_RANK, MLA_HEADS * (MLA_NOPE + MLA_V)), MLA_KV_RANK),
        'nsa_cmp_w1': w(ks[9], (L, 2, NSA_CMP_LEN * HEAD_DIM, NSA_CMP_HIDDEN), NSA_CMP_LEN * HEAD_DIM),
        'nsa_cmp_w2': w(ks[10], (L, 2, NSA_CMP_HIDDEN, HEAD_DIM), NSA_CMP_HIDDEN),
        'nsa_cmp_pos': 0.1 * jax.random.normal(ks[11], (L, 2, NSA_CMP_LEN, HEAD_DIM), f32),
        'w_out': w(ks[12], (L, MIX_WIDTH, D), MIX_WIDTH, DEEPNORM_BETA),
        'ln1_g': gain(ks[13], (L, D)),
        'ln1_b': bias(ks[14], (L, D)),
        'mem_wq': w(ks[15], (L, D, D), D),
        'mem_wkv': w(ks[16], (L, D, 2 * D), D),
        'mem_wo': w(ks[17], (L, D, D), D, DEEPNORM_BETA),
        'ln2_g': gain(ks[18], (L, D)),
        'ln2_b': bias(ks[19], (L, D)),
        'mlp_w1': w(ks[20], (L, D, D_FF), D),
        'mlp_w2': w(ks[21], (L, D_FF, D), D_FF, DEEPNORM_BETA),
        'ln3_g': gain(ks[22], (L, D)),
        'ln3_b': bias(ks[23], (L, D)),
    }


def reference(x, mem, ln_in_g, ln_in_b, w_in, mla_q_norm, mla_kv_norm, mla_w_uq, mla_w_ukv,
              nsa_cmp_w1, nsa_cmp_w2, nsa_cmp_pos, w_out, ln1_g, ln1_b,
              mem_wq, mem_wkv, mem_wo, ln2_g, ln2_b, mlp_w1, mlp_w2, ln3_g, ln3_b):
    h = layer_norm(x, ln_in_g, ln_in_b)
    for l in range(DEPTH):
        mix = hybrid_mixer(h, w_in[l], mla_q_norm[l], mla_kv_norm[l], mla_w_uq[l], mla_w_ukv[l],
                           nsa_cmp_w1[l], nsa_cmp_w2[l], nsa_cmp_pos[l], w_out[l])
        h = layer_norm(DEEPNORM_ALPHA * h + mix, ln1_g[l], ln1_b[l])
        h = layer_norm(DEEPNORM_ALPHA * h + memory_cross_attention(h, mem, mem_wq[l], mem_wkv[l], mem_wo[l]),
                       ln2_g[l], ln2_b[l])
        h = layer_norm(DEEPNORM_ALPHA * h + squared_relu_mlp(h, mlp_w1[l], mlp_w2[l]), ln3_g[l], ln3_b[l])
    return h
```

```python
from contextlib import ExitStack

import ml_dtypes
import numpy as np

import concourse.bass as bass
import concourse.mybir as mybir
from concourse.bass_utils import run_bass_kernel_spmd

F32 = mybir.dt.float32
BF16 = mybir.dt.bfloat16
AF = mybir.ActivationFunctionType
ALU = mybir.AluOpType
AX = mybir.AxisListType

D = 2048
S_ = 2048
NT = 16
DEPTH = 2
INW = 3916
DFF = 8192
ALPHA = (2 * DEPTH) ** 0.25
NEG = -30000.0
JOBLIM = None
PHASE_LOG = []


class Tk:
    __slots__ = ("w", "r", "excl")

    def __init__(self):
        self.w = None
        self.r = {}
        self.excl = False


class Buf:
    def __init__(self, h):
        self.h = h
        self.tk = Tk()

    def __getitem__(self, k):
        return self.h[k]


class Sched:
    def __init__(self, nc):
        self.nc = nc
        self.e = {"pe": nc.tensor, "act": nc.scalar, "dve": nc.vector, "pool": nc.gpsimd, "sp": nc.sync}
        self.nsem = 0
        self.psem = {k: self._sem() for k in self.e}
        self.pcnt = {k: 0 for k in self.e}
        self.last = {k: None for k in self.e}
        self.waited = {k: {} for k in self.e}
        self.dsem = {q: [[self._sem(), 0] for _ in range(n)] for q, n in (("sp", 16), ("pool", 16), ("act", 4))}
        self.drr = {q: 0 for q in self.dsem}
        self.ninst = 0

    def _sem(self):
        self.nsem += 1
        return self.nc.semaphore(f"sm{self.nsem}").__enter__()

    def need(self, eng, tok, same_ok):
        if tok is None:
            return
        sem, val, src = tok
        if src == eng and same_ok:
            return
        w = self.waited[eng]
        if w.get(sem.num, 0) >= val:
            return
        self.e[eng].wait_ge(sem, val)
        w[sem.num] = val

    def op(self, eng, fn, reads=(), writes=()):
        for t in reads:
            self.need(eng, t.w, eng == "pe")
            if t.excl:
                for tok in t.r.values():
                    self.need(eng, tok, True)
        for t in writes:
            self.need(eng, t.w, eng == "pe")
            for tok in t.r.values():
                self.need(eng, tok, eng == "pe")
        ins = fn(self.e[eng])
        if self.pcnt[eng] >= 30000:
            self.psem[eng] = self._sem()
            self.pcnt[eng] = 0
        self.pcnt[eng] += 1
        sem = self.psem[eng]
        ins.then_inc(sem, 1)
        tok = (sem, self.pcnt[eng], eng)
        self.last[eng] = tok
        for t in reads:
            t.r[eng] = tok
        for t in writes:
            t.w = tok
            t.r = {}
        self.ninst += 1
        return tok

    def dma(self, q, out, in_, reads=(), writes=()):
        for t in reads:
            self.need(q, t.w, False)
        for t in writes:
            self.need(q, t.w, False)
            for tok in t.r.values():
                self.need(q, tok, False)
        lst = self.dsem[q]
        i = self.drr[q]
        self.drr[q] = (i + 1) % len(lst)
        ent = lst[i]
        if ent[1] > 0:
            self.need(q, (ent[0], 16 * ent[1], "dma"), False)
        ent[1] += 1
        self.e[q].dma_start(out=out, in_=in_).then_inc(ent[0], 16)
        tok = (ent[0], 16 * ent[1], "dma")
        for t in reads:
            t.r[("d", ent[0].num)] = tok
        for t in writes:
            t.w = tok
            t.r = {}
        self.ninst += 1
        return tok

    def barrier(self):
        toks = [t for t in self.last.values() if t is not None]
        for q in self.dsem:
            for ent in self.dsem[q]:
                if ent[1] > 0:
                    toks.append((ent[0], 16 * ent[1], "dma"))
        for eng in self.e:
            for tok in toks:
                self.need(eng, tok, True)


class Rot:
    def __init__(self, bufs):
        self.bufs = bufs
        self.i = 0

    def next(self):
        b = self.bufs[self.i]
        self.i = (self.i + 1) % len(self.bufs)
        return b


def _rope_tab(dim):
    inv = (np.float32(500000.0) ** (-np.arange(0, dim, 2, dtype=np.float32) / np.float32(dim))).astype(np.float32)
    ang = (np.arange(S_, dtype=np.float32)[:, None] * inv[None, :]).astype(np.float32)
    cos = np.cos(ang).astype(np.float32).T
    sin = np.sin(ang).astype(np.float32).T
    C = np.concatenate([cos, cos], 0)
    Sn = np.concatenate([sin, sin], 0)
    half = dim // 2
    P = np.zeros((dim, dim), np.float32)
    for m in range(half):
        P[m + half, m] = -1.0
    for m in range(half, dim):
        P[m - half, m] = 1.0
    return np.ascontiguousarray(C), np.ascontiguousarray(Sn), P


def make_consts():
    bf = ml_dtypes.bfloat16
    c = {}
    c["ident"] = np.eye(128, dtype=np.float32).astype(bf)
    k = np.arange(128)[:, None]
    q = np.arange(128)[None, :]
    c["tri"] = np.where(k <= q, 0.0, NEG).astype(np.float32).astype(bf)
    c["anti"] = np.where(k > q, 0.0, NEG).astype(np.float32).astype(bf)
    C64, S64, P64 = _rope_tab(64)
    C32, S32, P32 = _rope_tab(32)
    c["rope64"] = np.stack([C64, S64], 1).astype(np.float32)
    c["rope32"] = np.stack([C32, S32], 1).astype(np.float32)
    c["p64"] = P64.astype(bf)
    c["p32"] = P32.astype(bf)
    t = np.arange(S_)
    n = np.arange(128)
    cm = ((16 * n[:, None] + 31) <= t[None, :]) & (n[:, None] < 127)
    c["cmpmask"] = np.where(cm, 0.0, NEG).astype(np.float32).astype(bf)
    starts = np.arange(127) * 16
    sel_start = np.arange(32) * 64
    ov = ((starts[:, None] < sel_start[None, :] + 64) & (starts[:, None] + 32 > sel_start[None, :])).astype(np.float32)
    ovf = np.zeros((128, 33), np.float32)
    ovf[:, 0] = 1.0
    ovf[:127, 1:] = ov
    c["ov"] = ovf.astype(bf)
    cur = t // 64
    j = np.arange(32)
    elig = j[None, :] <= cur[:, None]
    f0 = (j[None, :] == 0)
    f1 = (j[None, :] == cur[:, None])
    f2 = (j[None, :] == cur[:, None] - 1)
    forced = f0 | f1 | f2
    E = (elig & ~forced).astype(np.float32)
    Fv = np.where(~elig, -1e30, 0.0) + np.where(elig & f2, 1e4, 0.0) + np.where(elig & f1, 2e4, 0.0) + np.where(elig & f0, 4e4, 0.0)
    nsa = np.stack([E, Fv.astype(np.float32)], 1)
    c["nsatab"] = np.ascontiguousarray(nsa.reshape(16, 128, 2, 32).transpose(1, 0, 2, 3)).astype(np.float32)
    kk = np.arange(S_)
    c["ex"] = ((kk[None, :] // 64) == j[:, None]).astype(np.float32).astype(bf)
    curm = t // 256
    nb = np.arange(8)
    E2 = (nb[None, :] < curm[:, None]).astype(np.float32)
    F2 = np.where(nb[None, :] < curm[:, None], 0.0, -1e30).astype(np.float32)
    OWN = (nb[None, :] == curm[:, None]).astype(np.float32)
    mob = np.stack([E2, F2, OWN], 1)
    c["mobtab"] = np.ascontiguousarray(mob.reshape(16, 128, 3, 8).transpose(1, 0, 2, 3)).astype(np.float32)
    ex2 = np.zeros((8, 1024), np.float32)
    for b in range(8):
        ex2[b, b * 128:(b + 1) * 128] = 1.0
    c["ex2"] = ex2.astype(bf)
    return c


CONST_SPECS = [
    ("ident", [128, 128], BF16), ("tri", [128, 128], BF16), ("anti", [128, 128], BF16),
    ("rope64", [64, 2, S_], F32), ("rope32", [32, 2, S_], F32), ("p64", [64, 64], BF16), ("p32", [32, 32], BF16),
    ("cmpmask", [128, S_], BF16), ("ov", [128, 33], BF16), ("nsatab", [128, 16, 2, 32], F32), ("ex", [32, S_], BF16),
    ("mobtab", [128, 16, 3, 8], F32), ("ex2", [8, 1024], BF16),
]

W_SPECS = [
    ("ln_in_g", [D]), ("ln_in_b", [D]), ("w_in", [DEPTH, D, INW]), ("mla_q_norm", [DEPTH, 512]),
    ("mla_kv_norm", [DEPTH, 512]), ("mla_w_uq", [DEPTH, 512, 1536]), ("mla_w_ukv", [DEPTH, 512, 2048]),
    ("nsa_cmp_w1", [DEPTH, 2, 4096, 128]), ("nsa_cmp_w2", [DEPTH, 2, 128, 128]), ("nsa_cmp_pos", [DEPTH, 2, 32, 128]),
    ("w_out", [DEPTH, D, D]), ("ln1_g", [DEPTH, D]), ("ln1_b", [DEPTH, D]), ("mem_wq", [DEPTH, D, D]),
    ("mem_wkv", [DEPTH, D, 2 * D]), ("mem_wo", [DEPTH, D, D]), ("ln2_g", [DEPTH, D]), ("ln2_b", [DEPTH, D]),
    ("mlp_w1", [DEPTH, D, DFF]), ("mlp_w2", [DEPTH, DFF, D]), ("ln3_g", [DEPTH, D]), ("ln3_b", [DEPTH, D]),
]


def build_program(dbg=False, stop_after=None, depth=DEPTH):
    nc = bass.Bass("TRN2", target_bir_lowering=False)
    S = Sched(nc)
    skind = "ExternalOutput" if dbg else "Internal"

    def din(name, shape, dt=F32):
        return nc.dram_tensor(name, list(shape), dt, kind="ExternalInput").ap()

    def dscr(name, shape, dt):
        return nc.dram_tensor(name, list(shape), dt, kind=skind).ap()

    x_d = din("x", [S_, D])
    mem_d = din("mem", [256, D])
    W = {n: din(n, s) for n, s in W_SPECS}
    C = {n: din(n, s, dt) for n, s, dt in CONST_SPECS}
    out_d = nc.dram_tensor("out", [S_, D], F32, kind="ExternalOutput").ap()

    hres = [dscr(f"hres{i}", [S_, D], F32) for i in range(2)]
    hTd = [dscr(f"hT{i}", [D, S_], BF16) for i in range(2)]
    PT = dscr("PT", [25 * 128, S_], BF16)
    VT = dscr("VT", [S_, 768], BF16)
    NG = dscr("NG", [S_, 12], F32)
    CATT = dscr("CATT", [D, S_], BF16)
    XOT = dscr("XOT", [D, S_], BF16)
    MEMT = dscr("MEMT", [D, 256], BF16)
    WOUTB = nc.dram_tensor("WOUTB", [D, D], BF16).ap()
    WQB = nc.dram_tensor("WQB", [D, D], BF16).ap()
    WKVB = nc.dram_tensor("WKVB", [D, 2 * D], BF16).ap()
    WOB = nc.dram_tensor("WOB", [D, D], BF16).ap()
    W1B = nc.dram_tensor("W1B", [D, DFF], BF16).ap()
    W2B = nc.dram_tensor("W2B", [DFF, D], BF16).ap()
    precast_q = []

    def precast_plan(l):
        jobs = []
        for dst, src, rows in ((WOUTB, W["w_out"][l], 512), (WKVB, W["mem_wkv"][l], 256), (WQB, W["mem_wq"][l], 512),
                               (WOB, W["mem_wo"][l], 512), (W1B, W["mlp_w1"][l], 128), (W2B, W["mlp_w2"][l], 512)):
            n = dst.shape[0]
            for r0 in range(0, n, rows):
                jobs.append((dst[r0:r0 + rows, :], src[r0:r0 + rows, :]))
        precast_q.extend(jobs)

    def precast(n):
        for _ in range(n):
            if precast_q:
                d_, s_ = precast_q.pop(0)
                S.dma("pool", d_, s_)

    es_glob = ExitStack()

    uid = [0]

    def sb(es, name, shape, dt):
        uid[0] += 1
        return Buf(es.enter_context(nc.sbuf_tensor(f"sb{uid[0]}_{name}", list(shape), dt)))

    ps = [Buf(es_glob.enter_context(nc.psum_tensor(f"ps{i}", [128, 512], F32))) for i in range(8)]
    for _b in ps:
        _b.tk.excl = True

    class PTBank:
        def __init__(self, b):
            self.b = b
            self.tk = b.tk
            self.v = b[:, :].bitcast(BF16)

        def ap(self, a, b):
            return self.v[:, a:b]

        def pap(self, p, a, b):
            return self.v[0:p, a:b]

    ptrot = Rot([PTBank(ps[6]), PTBank(ps[7])])
    ptrot_raw = Rot([ps[6], ps[7]])

    ident = sb(es_glob, "ident", [128, 128], BF16)
    tri = sb(es_glob, "tri", [128, 128], BF16)
    anti = sb(es_glob, "anti", [128, 128], BF16)
    S.dma("sp", ident[:], C["ident"], writes=[ident.tk])
    S.dma("sp", tri[:], C["tri"], writes=[tri.tk])
    S.dma("sp", anti[:], C["anti"], writes=[anti.tk])
    small = sb(es_glob, "small", [128, 64], F32)
    small_tk = [Tk() for _ in range(64)]
    small_i = [0]

    acc0 = sb(es_glob, "acc0", [128, 512], F32)
    acc0_tok = S.op("dve", lambda e: e.memset(acc0[:, :], 0.0), writes=[acc0.tk])
    acc_i = [0]

    def zcol():
        i = acc_i[0]
        acc_i[0] = i + 1
        assert i < 512
        tk = Tk()
        tk.w = acc0_tok
        return acc0[:, i:i + 1], tk

    def sm():
        i = small_i[0]
        small_i[0] = (i + 1) % 64
        return small[:, i:i + 1], small_tk[i]

    def mm(out, lhsT, rhs, start, stop, reads, writes):
        return S.op("pe", lambda e: e.matmul(out, lhsT, rhs, start=start, stop=stop), reads=reads, writes=writes)

    def tp(out, in_, reads, writes, kp=128):
        return S.op("pe", lambda e: e.transpose(out, in_, ident[0:kp, 0:kp]), reads=list(reads) + [ident.tk], writes=writes)

    def act(out, in_, func, reads, writes, **kw):
        return S.op("act", lambda e: e.activation(out, in_, func, **kw), reads=reads, writes=writes)

    class LNCtx:
        pass

    class _B2:
        def __init__(self, ap, tk):
            self.ap_ = ap
            self.tk = tk

        def __getitem__(self, k):
            if k == slice(None):
                return self.ap_
            return self.ap_[k]

    def ln_setup(es, g_ap, b_ap, slim=False):
        L = LNCtx()
        L.store_q = "sp"
        L.cast_eng = "act"
        nb = 2
        L.gB = sb(es, "ln_gB", [128, D], F32)
        L.bB = sb(es, "ln_bB", [128, D], F32)
        S.dma("sp", L.gB[:], g_ap.partition_broadcast(128), writes=[L.gB.tk])
        S.dma("sp", L.bB[:], b_ap.partition_broadcast(128), writes=[L.bB.tk])
        L.hin = Rot([sb(es, f"ln_hin{i}", [128, D], F32) for i in range(2)])
        L.y = Rot([sb(es, f"ln_y{i}", [128, D], F32) for i in range(nb)])
        L.junk = sb(es, "ln_junk", [128, D], BF16)
        L.hb = Rot([sb(es, f"ln_hb{i}", [128, D], BF16) for i in range(2)])
        L.hTt = Rot([sb(es, f"ln_hTt{i}", [128, 16, 512], BF16) for i in range(2)])
        L.cur_hTt = None
        return L

    def ln_prefetch(L, hin_d, t):
        hb = L.hin.next()
        S.dma("sp", hb[:], hin_d[t * 128:(t + 1) * 128, :], writes=[hb.tk])
        return hb

    def ln_tile(L, t, hin_buf, psb, hout_d, hTout_d):
        y = L.y.next()
        if psb is None:
            ysrc = hin_buf
        else:
            for c4 in range(4):
                S.op("dve", lambda e: e.scalar_tensor_tensor(
                    out=y[:, c4 * 512:(c4 + 1) * 512], in0=hin_buf[:, c4 * 512:(c4 + 1) * 512], scalar=float(ALPHA),
                    in1=psb[c4][:, :], op0=ALU.mult, op1=ALU.add),
                    reads=[hin_buf.tk, psb[c4].tk], writes=[y.tk])
            ysrc = y
        return ln_core(L, t, ysrc[:], ysrc.tk, y[:], y.tk, hout_d, hTout_d)

    def ln_core(L, t, ysrc_ap, ysrc_tk, y_ap, y_tk, hout_d, hTout_d):
        y = _B2(y_ap, y_tk)
        ysrc = _B2(ysrc_ap, ysrc_tk)
        s1, s1k = zcol()
        s2, s2k = zcol()
        act(L.junk[:], ysrc[:], AF.Copy, [ysrc.tk, s1k], [L.junk.tk, s1k], accum_out=s1)
        act(L.junk[:], ysrc[:], AF.Square, [ysrc.tk, s2k], [L.junk.tk, s2k], accum_out=s2)
        mean, meank = sm()
        S.op("dve", lambda e: e.tensor_scalar(mean, s1, 1.0 / D, None, op0=ALU.mult), reads=[s1k], writes=[meank])
        msq, msqk = sm()
        S.op("dve", lambda e: e.tensor_tensor(msq, mean, mean, ALU.mult), reads=[meank], writes=[msqk])
        var, vark = sm()
        S.op("dve", lambda e: e.scalar_tensor_tensor(out=var, in0=s2, scalar=1.0 / D, in1=msq, op0=ALU.mult,
                                                     op1=ALU.subtract), reads=[s2k, msqk], writes=[vark])
        S.op("dve", lambda e: e.tensor_scalar(var, var, 1e-5, None, op0=ALU.add), reads=[vark], writes=[vark])
        sd, sdk = sm()
        act(sd, var, AF.Sqrt, [vark], [sdk])
        rstd, rstdk = sm()
        S.op("dve", lambda e: e.reciprocal(rstd, sd), reads=[sdk], writes=[rstdk])
        S.op("dve", lambda e: e.scalar_tensor_tensor(out=y[:], in0=ysrc[:], scalar=mean, in1=L.gB[:], op0=ALU.subtract,
                                                     op1=ALU.mult), reads=[ysrc.tk, meank, L.gB.tk], writes=[y.tk])
        S.op("dve", lambda e: e.scalar_tensor_tensor(out=y[:], in0=y[:], scalar=rstd, in1=L.bB[:], op0=ALU.mult,
                                                     op1=ALU.add), reads=[y.tk, rstdk, L.bB.tk], writes=[y.tk])
        S.dma(L.store_q, hout_d[t * 128:(t + 1) * 128, :], y[:], reads=[y.tk])
        if hTout_d is None:
            return lambda: None
        hb = L.hb.next()
        if L.cast_eng == "act":
            act(hb[:], y[:], AF.Copy, [y.tk], [hb.tk])
        else:
            S.op("pool", lambda e: e.tensor_copy(hb[:], y[:]), reads=[y.tk], writes=[hb.tk])
        if t % 4 == 0 and L.hTt is not None:
            L.cur_hTt = L.hTt.next()
        hTt = L.cur_hTt
        tq = t % 4

        def fin():
            for g2 in range(2):
                pt = ptrot.next()
                for j in range(8):
                    kt = g2 * 8 + j
                    tp(pt.ap(j * 128, (j + 1) * 128), hb[:, kt * 128:(kt + 1) * 128], [hb.tk], [pt.tk])
                S.op("dve", lambda e: e.tensor_copy(
                    hTt[:, g2 * 8:(g2 + 1) * 8, tq * 128:(tq + 1) * 128],
                    pt.ap(0, 1024).rearrange("p (j q) -> p j q", j=8)), reads=[pt.tk], writes=[hTt.tk])
            if tq == 3:
                t0 = (t - 3) * 128
                S.dma(L.store_q, hTout_d[:, t0:t0 + 512].rearrange("(kt p) q -> p kt q", p=128), hTt[:], reads=[hTt.tk])
        return fin

    def phase_ln_in():
        with ExitStack() as es:
            L = ln_setup(es, W["ln_in_g"], W["ln_in_b"])
            nxt = ln_prefetch(L, x_d, 0)
            pend = None
            for t in range(NT):
                cur = nxt
                if t + 1 < NT:
                    nxt = ln_prefetch(L, x_d, t + 1)
                f = ln_tile(L, t, cur, None, hres[0], hTd[0])
                if pend is not None:
                    pend()
                pend = f
            pend()
            S.barrier()

    def load_rope(es, which):
        R = 64 if which == 64 else 32
        tab = sb(es, f"rope{R}", [R, 2, S_], F32)
        pm = sb(es, f"pm{R}", [R, R], BF16)
        S.dma("sp", tab[:], C[f"rope{R}"], writes=[tab.tk])
        S.dma("sp", pm[:], C[f"p{R}"], writes=[pm.tk])
        return (R, tab, pm)

    def mk_rope_scratch(es):
        return (Rot([sb(es, f"rt1_{i}", [64, 512], F32) for i in range(2)]),
                Rot([sb(es, f"rt2_{i}", [64, 512], F32) for i in range(2)]))

    def apply_rope(dst, dst_tk, rp, pb, c, rsc, roprot):
        R, tab, pm = rp
        pr = roprot.next()
        mm(pr[0:R, :], pm[:, :], dst, True, True, [pm.tk, dst_tk], [pr.tk])
        t1 = rsc[0].next()
        t2 = rsc[1].next()
        S.op("dve", lambda e: e.tensor_tensor(t1[0:R, :], pb[0:R, :], tab[:, 0, c * 512:(c + 1) * 512], ALU.mult),
             reads=[pb.tk, tab.tk], writes=[t1.tk])
        S.op("dve", lambda e: e.tensor_tensor(t2[0:R, :], pr[0:R, :], tab[:, 1, c * 512:(c + 1) * 512], ALU.mult),
             reads=[pr.tk, tab.tk], writes=[t2.tk])
        S.op("pool", lambda e: e.tensor_tensor(dst, t1[0:R, :], t2[0:R, :], ALU.add),
             reads=[t1.tk, t2.tk], writes=[dst_tk])

    deferred_outs = []

    def flush_outs():
        while deferred_outs:
            deferred_outs.pop(0)()

    def attention(c, kts, qk, vaug, v_reads, mode, sbanks, pbufs, obanks, scale, on_final, K=128, hook=None):
        items = []
        for kt in kts:
            lo = max(0, kt - 4 * c)
            hi = 3 if mode == "causal" else min(3, kt + 4 - 4 * c)
            if lo <= hi:
                items.append((kt, lo, hi))

        def stage_a(it):
            kt, lo, hi = it
            c0, c1 = lo * 128, (hi + 1) * 128
            sp = sbanks.next()
            mms = qk(kt, c * 512 + c0, c * 512 + c1)
            masks = []
            for qs in range(lo, hi + 1):
                rel = 4 * c + qs - kt
                m = tri if rel == 0 else (anti if (mode == "win" and rel == 4) else None)
                if m is not None:
                    masks.append((qs, m))
            n = len(mms) + len(masks)
            for i, (l, r, rd) in enumerate(mms):
                mm(sp[0:K, c0:c1], l, r, i == 0, i == n - 1, rd, [sp.tk])
            for j, (qs, m) in enumerate(masks):
                mm(sp[0:K, qs * 128:(qs + 1) * 128], ident[:, 0:K], m[:, :], False, len(mms) + j == n - 1,
                   [ident.tk, m.tk], [sp.tk])
            return sp

        def stage_b(it, sp):
            kt, lo, hi = it
            c0, c1 = lo * 128, (hi + 1) * 128
            pb = pbufs.next()
            act(pb[0:K, c0:c1], sp[0:K, c0:c1], AF.Exp, [sp.tk], [pb.tk], scale=float(scale))
            for qs in range(lo, hi + 1):
                first_kt = 0 if mode == "causal" else max(0, 4 * c + qs - 4)
                last_kt = 4 * c + qs
                ob = obanks[qs]
                mm(ob[:, 0:129], pb[0:K, qs * 128:(qs + 1) * 128], vaug(kt), kt == first_kt, kt == last_kt,
                   [pb.tk] + v_reads, [ob.tk])
                if kt == last_kt:
                    on_final(qs, ob)

        sp_next = stage_a(items[0])
        for i, it in enumerate(items):
            sp = sp_next
            if i + 1 < len(items):
                sp_next = stage_a(items[i + 1])
            stage_b(it, sp)
            if i == 0:
                flush_outs()
            if hook is not None:
                hook()

    def phase_proj(l, hT_src):
        with ExitStack() as es:
            hT = sb(es, "hT", [128, 16, S_], BF16)
            hT_tk = [Tk() for _ in range(4)]
            for g in range(4):
                S.dma("sp", hT[:, :, g * 512:(g + 1) * 512],
                      hT_src[:, g * 512:(g + 1) * 512].rearrange("(kt p) t -> p kt t", p=128), writes=[hT_tk[g]])
            wrot = Rot([sb(es, f"wsl{i}", [128, 16, 512], BF16) for i in range(2)])
            rp32 = load_rope(es, 32)
            rp64 = load_rope(es, 64)
            rsc = mk_rope_scratch(es)
            stage_rot = Rot([(sb(es, f"stg{i}", [128, S_], BF16), [Tk() for _ in range(4)]) for i in range(2)])
            vstage = Rot([sb(es, f"vst{i}", [128, 512], BF16) for i in range(2)])
            cn_rot = Rot([sb(es, f"cn{i}", [128, 512], BF16) for i in range(2)])
            cst_rot = Rot([sb(es, f"cst{i}", [128, 4, 512], BF16) for i in range(2)])
            junk = sb(es, "pjunk", [128, 512], BF16)
            ngs = sb(es, "ngs", [128, 16, 12], F32)
            gB = Rot([sb(es, f"gBq{i}", [128, 512], F32) for i in range(2)])
            psrot = Rot(ps[0:4])
            roprot = Rot(ps[4:6])

            def load_slab(c0, n):
                wb = wrot.next()
                S.dma("pool", wb[:, :, 0:n], W["w_in"][l, :, c0:c0 + n].rearrange("(kt p) n -> p kt n", p=128),
                      writes=[wb.tk])
                return wb

            jc = [0]

            def feat_job(wb, wc0, M, rp, pt_tile):
                jc[0] += 1
                if JOBLIM is not None and jc[0] > JOBLIM:
                    return
                stage, stk = stage_rot.next()
                pend = None
                for c in range(4):
                    pb = psrot.next()
                    for kt in range(16):
                        mm(pb[0:M, :], wb[:, kt, wc0:wc0 + M], hT[:, kt, c * 512:(c + 1) * 512], kt == 0, kt == 15,
                           [wb.tk, hT_tk[c]], [pb.tk])
                    act(stage[0:M, c * 512:(c + 1) * 512], pb[0:M, :], AF.Copy, [pb.tk], [stk[c]])
                    if pend is not None:
                        pend()
                        pend = None
                    if rp is not None:
                        def mk(c=c, pb=pb):
                            R = rp[0]
                            return lambda: apply_rope(stage[0:R, c * 512:(c + 1) * 512], stk[c], rp, pb, c, rsc, roprot)
                        pend = mk()
                if pend is not None:
                    pend()
                S.dma("sp", PT[pt_tile * 128:pt_tile * 128 + M, :], stage[0:M, :], reads=stk)

            def tok_job(wb, wc0, n, kind, dest, g_ap=None):
                jc[0] += 1
                if JOBLIM is not None and jc[0] > JOBLIM:
                    return
                if kind == "rms":
                    g = gB.next()
                    S.dma("sp", g[:], g_ap.partition_broadcast(128), writes=[g.tk])
                cst = None
                pend_fin = [None]
                for t in range(NT):
                    pb = psrot.next()
                    for kt in range(16):
                        mm(pb[:, 0:n], hT[:, kt, t * 128:(t + 1) * 128], wb[:, kt, wc0:wc0 + n], kt == 0, kt == 15,
                           [hT_tk[t // 4], wb.tk], [pb.tk])
                    if kind == "v":
                        vs = vstage.next()
                        act(vs[:, 0:n], pb[:, 0:n], AF.Copy, [pb.tk], [vs.tk])
                        S.dma("sp", VT[t * 128:(t + 1) * 128, dest:dest + n], vs[:, 0:n], reads=[vs.tk])
                    elif kind == "gate":
                        act(ngs[:, t, :], pb[:, 0:12], AF.Sigmoid, [pb.tk], [ngs.tk])
                    else:
                        ss, ssk = zcol()
                        act(junk[:, :], pb[:, :], AF.Square, [pb.tk, ssk], [junk.tk, ssk], accum_out=ss)
                        S.op("dve", lambda e: e.tensor_scalar(ss, ss, 1.0 / 512, 1e-6, op0=ALU.mult, op1=ALU.add),
                             reads=[ssk], writes=[ssk])
                        sd, sdk = sm()
                        act(sd, ss, AF.Sqrt, [ssk], [sdk])
                        r, rk = sm()
                        S.op("dve", lambda e: e.reciprocal(r, sd), reads=[sdk], writes=[rk])
                        cn = cn_rot.next()
                        S.op("dve", lambda e: e.scalar_tensor_tensor(out=cn[:, :], in0=pb[:, :], scalar=r, in1=g[:, :],
                                                                     op0=ALU.mult, op1=ALU.mult),
                             reads=[pb.tk, rk, g.tk], writes=[cn.tk])
                        tq = t % 4
                        if tq == 0:
                            cst = cst_rot.next()

                        def fin(cn=cn, cst=cst, tq=tq, t=t):
                            pt = ptrot.next()
                            for j in range(4):
                                tp(pt.ap(j * 128, (j + 1) * 128), cn[:, j * 128:(j + 1) * 128], [cn.tk], [pt.tk])
                            S.op("dve", lambda e: e.tensor_copy(cst[:, :, tq * 128:(tq + 1) * 128],
                                                                pt.ap(0, 512).rearrange("p (j q) -> p j q", j=4)),
                                 reads=[pt.tk], writes=[cst.tk])
                            if tq == 3:
                                t0 = (t - 3) * 128
                                S.dma("sp", PT[dest * 128:(dest + 4) * 128, t0:t0 + 512].rearrange("(j p) q -> p j q", p=128),
                                      cst[:], reads=[cst.tk])
                        if pend_fin[0] is not None:
                            pend_fin[0]()
                        pend_fin[0] = fin
                if pend_fin[0] is not None:
                    pend_fin[0]()
                    pend_fin[0] = None

            wb = load_slab(0, 512)
            wb2 = load_slab(512, 512)
            tok_job(wb, 0, 512, "rms", 0, W["mla_q_norm"][l])
            wb = load_slab(1024, 64)
            tok_job(wb2, 0, 512, "rms", 4, W["mla_kv_norm"][l])
            wb2 = load_slab(1088, 512)
            feat_job(wb, 0, 64, rp64, 8)
            wb = load_slab(1600, 512)
            for h in range(4):
                feat_job(wb2, h * 128, 128, rp32, 9 + h)
            wb2 = load_slab(2112, 268)
            feat_job(wb, 0, 128, rp32, 13)
            feat_job(wb, 128, 128, None, 14)
            feat_job(wb, 256, 128, rp32, 15)
            tok_job(wb, 384, 128, "v", 0)
            wb = load_slab(2380, 512)
            feat_job(wb2, 0, 128, rp32, 16)
            tok_job(wb2, 128, 128, "v", 128)
            tok_job(wb2, 256, 12, "gate", 0)
            S.dma("sp", NG.rearrange("(t p) g -> p t g", p=128), ngs[:], reads=[ngs.tk])
            wb2 = load_slab(2892, 512)
            for h in range(4):
                feat_job(wb, h * 128, 128, rp32, 17 + h)
            wb = load_slab(3404, 512)
            for h in range(4):
                feat_job(wb2, h * 128, 128, rp32, 21 + h)
            tok_job(wb, 0, 512, "v", 256)
            S.barrier()

    def emit_head_out(onb, c, row0, oTrot, get=None, pr=None):
        pt = (pr or ptrot).next()
        for qs in range(4):
            tp(pt.ap(qs * 128, (qs + 1) * 128), onb[:, qs, :] if get is None else get(qs), [onb.tk], [pt.tk])
        oT = oTrot.next()
        S.op("dve", lambda e: e.tensor_copy(oT[:, :], pt.ap(0, 512)), reads=[pt.tk], writes=[oT.tk])
        S.dma("sp", CATT[row0:row0 + 128, c * 512:(c + 1) * 512], oT[:, :], reads=[oT.tk])

    def phase_mla(l):
        with ExitStack() as es:
            cqT = sb(es, "cqT", [128, 4, S_], BF16)
            ckT = sb(es, "ckT", [128, 4, S_], BF16)
            S.dma("sp", cqT[:], PT[0:512, :].rearrange("(kt p) t -> p kt t", p=128), writes=[cqT.tk])
            S.dma("sp", ckT[:], PT[512:1024, :].rearrange("(kt p) t -> p kt t", p=128), writes=[ckT.tk])
            kr = sb(es, "kr", [64, S_], BF16)
            S.dma("sp", kr[:], PT[1024:1088, :], writes=[kr.tk])
            wuq = sb(es, "wuq", [128, 4, 1536], BF16)
            wukv = sb(es, "wukv", [128, 4, 2048], BF16)
            S.dma("pool", wuq[:], W["mla_w_uq"][l].rearrange("(kt p) n -> p kt n", p=128), writes=[wuq.tk])
            S.dma("pool", wukv[:], W["mla_w_ukv"][l].rearrange("(kt p) n -> p kt n", p=128), writes=[wukv.tk])
            rp64 = load_rope(es, 64)
            rsc = mk_rope_scratch(es)
            heads = Rot([dict(qn=sb(es, f"qn{i}", [128, S_], BF16), qr=sb(es, f"qr{i}", [64, S_], BF16),
                              kn=sb(es, f"kn{i}", [128, S_], BF16), vh=sb(es, f"vh{i}", [128, 16, 129], BF16),
                              qrk=[Tk() for _ in range(4)]) for i in range(2)])
            for hb in heads.bufs:
                S.op("pool", lambda e: e.memset(hb["vh"][:, :, 128:129], 1.0), writes=[hb["vh"].tk])
            pbufs = Rot([sb(es, f"pT{i}", [128, 512], BF16) for i in range(3)])
            onbs = Rot([sb(es, f"onb{i}", [128, 4, 128], BF16) for i in range(2)])
            oTrot = Rot([sb(es, f"oT{i}", [128, 512], BF16) for i in range(2)])
            sbanks = Rot(ps[0:2])
            obanks = ps[2:6]
            projrot = Rot(ps[0:6])
            scale = 192.0 ** -0.5
            precast_plan(l)
            for h in range(8):
                precast(2)
                H = heads.next()
                qn, qr, kn, vh, qrk = H["qn"], H["qr"], H["kn"], H["vh"], H["qrk"]
                for c in range(4):
                    cs = slice(c * 512, (c + 1) * 512)
                    pbr = projrot.next()
                    for kt in range(4):
                        mm(pbr[0:64, :], wuq[:, kt, 192 * h + 128:192 * h + 192], cqT[:, kt, cs], kt == 0, kt == 3,
                           [wuq.tk, cqT.tk], [pbr.tk])
                    act(qr[:, cs], pbr[0:64, :], AF.Copy, [pbr.tk], [qrk[c]])
                    pb = projrot.next()
                    for kt in range(4):
                        mm(pb[:, :], wuq[:, kt, 192 * h:192 * h + 128], cqT[:, kt, cs], kt == 0, kt == 3,
                           [wuq.tk, cqT.tk], [pb.tk])
                    act(qn[:, cs], pb[:, :], AF.Copy, [pb.tk], [qn.tk])
                    pb = projrot.next()
                    for kt in range(4):
                        mm(pb[:, :], wukv[:, kt, 256 * h:256 * h + 128], ckT[:, kt, cs], kt == 0, kt == 3,
                           [wukv.tk, ckT.tk], [pb.tk])
                    act(kn[:, cs], pb[:, :], AF.Copy, [pb.tk], [kn.tk])
                    apply_rope(qr[:, cs], qrk[c], rp64, pbr, c, rsc, ptrot_raw)
                for t in range(NT):
                    pb = projrot.next()
                    for kt in range(4):
                        mm(pb[:, 0:128], ckT[:, kt, t * 128:(t + 1) * 128], wukv[:, kt, 256 * h + 128:256 * h + 256],
                           kt == 0, kt == 3, [wukv.tk, ckT.tk], [pb.tk])
                    act(vh[:, t, 0:128], pb[:, 0:128], AF.Copy, [pb.tk], [vh.tk])

                def qk(kt, q0, q1):
                    return [(kn[:, kt * 128:(kt + 1) * 128], qn[:, q0:q1], [kn.tk, qn.tk]),
                            (kr[:, kt * 128:(kt + 1) * 128], qr[:, q0:q1], [kr.tk] + qrk)]

                for c in range(4):
                    onb = onbs.next()

                    def on_final(qs, ob):
                        rv, rvk = sm()
                        S.op("dve", lambda e: e.reciprocal(rv, ob[:, 128:129]), reads=[ob.tk], writes=[rvk])
                        S.op("dve", lambda e: e.tensor_scalar(onb[:, qs, :], ob[:, 0:128], rv, None, op0=ALU.mult),
                             reads=[ob.tk, rvk], writes=[onb.tk])
                        if qs == 3:
                            deferred_outs.append(lambda onb=onb, c=c, h=h: emit_head_out(onb, c, h * 128, oTrot))

                    attention(c, range(0, 4 * c + 4), qk, lambda kt: vh[:, kt, :], [vh.tk], "causal", sbanks, pbufs,
                              obanks, scale, on_final)
            flush_outs()
            S.barrier()

    def phase_nsa(l):
        with ExitStack() as es:
            nqT = sb(es, "nqT", [128, 4, S_], BF16)
            S.dma("sp", nqT[:], PT[9 * 128:13 * 128, :].rearrange("(h p) t -> p h t", p=128), writes=[nqT.tk])
            srcs = []
            for i, tl in enumerate((13, 14, 15, 16)):
                b = sb(es, f"nsrc{i}", [128, S_], BF16)
                S.dma("sp", b[:], PT[tl * 128:(tl + 1) * 128, :], writes=[b.tk])
                srcs.append(b)
            kcsrc, vcsrc, ksT, kwT = srcs
            vs = sb(es, "vs", [128, 16, 129], BF16)
            vw = sb(es, "vw", [128, 16, 129], BF16)
            S.dma("sp", vs[:, :, 0:128], VT[:, 0:128].rearrange("(t p) d -> p t d", p=128), writes=[vs.tk])
            S.dma("sp", vw[:, :, 0:128], VT[:, 128:256].rearrange("(t p) d -> p t d", p=128), writes=[vw.tk])
            S.op("pool", lambda e: e.memset(vs[:, :, 128:129], 1.0), writes=[vs.tk])
            S.op("pool", lambda e: e.memset(vw[:, :, 128:129], 1.0), writes=[vw.tk])
            w1b = [sb(es, f"cw1_{i}", [128, 32, 128], BF16) for i in range(2)]
            w2b = [sb(es, f"cw2_{i}", [128, 128], BF16) for i in range(2)]
            posb = [sb(es, f"cpos_{i}", [32, 128], BF16) for i in range(2)]
            posT = [sb(es, f"cposT_{i}", [128, 32], BF16) for i in range(2)]
            for i in range(2):
                S.dma("pool", w1b[i][:], W["nsa_cmp_w1"][l, i].rearrange("(l d) j -> d l j", d=128), writes=[w1b[i].tk])
                S.dma("pool", w2b[i][:], W["nsa_cmp_w2"][l, i], writes=[w2b[i].tk])
                S.dma("pool", posb[i][:], W["nsa_cmp_pos"][l, i], writes=[posb[i].tk])
            cmask = sb(es, "cmask", [128, S_], BF16)
            S.dma("sp", cmask[:], C["cmpmask"], writes=[cmask.tk])
            ntab = sb(es, "ntab", [128, 16, 2, 32], F32)
            S.dma("sp", ntab[:], C["nsatab"], writes=[ntab.tk])
            ex = sb(es, "ex", [32, S_], BF16)
            S.dma("sp", ex[:], C["ex"], writes=[ex.tk])
            ng = sb(es, "ng", [128, 16, 12], F32)
            S.dma("sp", ng[:], NG.rearrange("(t p) g -> p t g", p=128), writes=[ng.tk])
            kcT = sb(es, "kcT", [128, 128], BF16)
            vcaug = sb(es, "vcaug", [128, 161], BF16)
            S.dma("sp", vcaug[:, 128:161], C["ov"], writes=[vcaug.tk])
            impS = sb(es, "impS", [128, 16, 32], F32)
            acc_rot = Rot([sb(es, f"acc{i}", [128, 4, 4, 128], F32) for i in range(2)])
            accb_rot = Rot([sb(es, f"accb{i}", [128, 4, 4, 128], BF16) for i in range(2)])
            biasT_rot = Rot([sb(es, f"biasT{i}", [32, 512], BF16) for i in range(2)])
            pbufs = Rot([sb(es, f"pT{i}", [128, 512], BF16) for i in range(3)])
            oTrot = Rot([sb(es, f"oT{i}", [128, 512], BF16) for i in range(2)])
            xg = sb(es, "xg", [128, 128], F32)
            ug = sb(es, "ug", [128, 128], F32)
            hidb = [sb(es, f"hidb{i}", [128, 128], BF16) for i in range(2)]
            scr = Rot([sb(es, f"scr{i}", [128, 32], F32) for i in range(2)])
            scr2 = Rot([sb(es, f"scr2_{i}", [128, 32], F32) for i in range(2)])
            mx = Rot([sb(es, f"mx{i}", [128, 16], F32) for i in range(2)])
            bsb = [sb(es, f"bsb{i}", [128, 32], BF16) for i in range(4)]
            sbanks = Rot(ps[0:2])
            obanks = ps[2:6]
            scale = 128.0 ** -0.5

            for i in range(2):
                pt = ptrot.next()
                tp(pt.pap(128, 0, 32), posb[i][:, :], [posb[i].tk], [pt.tk], kp=32)
                S.op("dve", lambda e: e.tensor_copy(posT[i][:, :], pt.ap(0, 32)), reads=[pt.tk], writes=[posT[i].tk])
                src = kcsrc if i == 0 else vcsrc
                v3 = src[:, :].rearrange("p (n s) -> p n s", s=16)
                pa = sbanks.next()
                for ll in range(32):
                    rhs = v3[:, 0:127, ll] if ll < 16 else v3[:, 1:128, ll - 16]
                    mm(pa[:, 0:127], w1b[i][:, ll, :], rhs, ll == 0, ll == 31, [w1b[i].tk, src.tk], [pa.tk])
                pbk = sbanks.next()
                for ll in range(32):
                    mm(pbk[:, 0:1], w1b[i][:, ll, :], posT[i][:, ll:ll + 1], ll == 0, ll == 31,
                       [w1b[i].tk, posT[i].tk], [pbk.tk])
                cst, cstk = sm()
                S.op("dve", lambda e: e.tensor_copy(cst, pbk[:, 0:1]), reads=[pbk.tk], writes=[cstk])
                S.op("dve", lambda e: e.tensor_scalar(xg[:, 0:127], pa[:, 0:127], cst, None, op0=ALU.add),
                     reads=[pa.tk, cstk], writes=[xg.tk])
                S.op("dve", lambda e: e.tensor_tensor(ug[:, 0:127], xg[:, 0:127], xg[:, 0:127], ALU.mult),
                     reads=[xg.tk], writes=[ug.tk])
                S.op("dve", lambda e: e.tensor_scalar(ug[:, 0:127], ug[:, 0:127], 0.0713548162726, 1.5957691216057308,
                                                      op0=ALU.mult, op1=ALU.add), reads=[ug.tk], writes=[ug.tk])
                S.op("dve", lambda e: e.tensor_tensor(ug[:, 0:127], ug[:, 0:127], xg[:, 0:127], ALU.mult),
                     reads=[ug.tk, xg.tk], writes=[ug.tk])
                act(ug[:, 0:127], ug[:, 0:127], AF.Sigmoid, [ug.tk], [ug.tk])
                S.op("dve", lambda e: e.tensor_tensor(hidb[i][:, 0:127], ug[:, 0:127], xg[:, 0:127], ALU.mult),
                     reads=[ug.tk, xg.tk], writes=[hidb[i].tk])
                po = sbanks.next()
                if i == 0:
                    mm(po[:, 0:127], w2b[0][:, :], hidb[0][:, 0:127], True, True, [w2b[0].tk, hidb[0].tk], [po.tk])
                    act(kcT[:, 0:127], po[:, 0:127], AF.Copy, [po.tk], [kcT.tk])
                else:
                    mm(po[0:127, 0:128], hidb[1][:, 0:127], w2b[1][:, :], True, True, [w2b[1].tk, hidb[1].tk], [po.tk])
                    act(vcaug[0:127, 0:128], po[0:127, 0:128], AF.Copy, [po.tk], [vcaug.tk])

            for c in range(4):
                cs = slice(c * 512, (c + 1) * 512)
                acc = acc_rot.next()
                cmpb = [ps[6], ps[7]]

                def cmp_head(h):
                    sp = sbanks.next()
                    mm(sp[0:127, :], kcT[:, 0:127], nqT[:, h, cs], True, False, [kcT.tk, nqT.tk], [sp.tk])
                    mm(sp[0:127, :], ident[:, 0:127], cmask[:, cs], False, True, [ident.tk, cmask.tk], [sp.tk])
                    pb = pbufs.next()
                    act(pb[0:127, :], sp[0:127, :], AF.Exp, [sp.tk], [pb.tk], scale=float(scale))
                    for qs in range(4):
                        T = 4 * c + qs
                        ob = cmpb[qs % 2]
                        mm(ob[:, 0:161], pb[0:127, qs * 128:(qs + 1) * 128], vcaug[0:127, 0:161], True, True,
                           [pb.tk, vcaug.tk], [ob.tk])
                        rv, rvk = sm()
                        S.op("dve", lambda e: e.tensor_scalar(rv, ob[:, 128:129], 1e-30, None, op0=ALU.max),
                             reads=[ob.tk], writes=[rvk])
                        S.op("dve", lambda e: e.reciprocal(rv, rv), reads=[rvk], writes=[rvk])
                        cf, cfk = sm()
                        S.op("dve", lambda e: e.tensor_tensor(cf, rv, ng[:, T, 3 * h:3 * h + 1], ALU.mult),
                             reads=[rvk, ng.tk], writes=[cfk])
                        S.op("dve", lambda e: e.tensor_scalar(acc[:, qs, h, :], ob[:, 0:128], cf, None, op0=ALU.mult),
                             reads=[ob.tk, cfk], writes=[acc.tk])
                        if h == 0:
                            S.op("dve", lambda e: e.tensor_scalar(impS[:, T, :], ob[:, 129:161], rv, None, op0=ALU.mult),
                                 reads=[ob.tk, rvk], writes=[impS.tk])
                        else:
                            S.op("dve", lambda e: e.scalar_tensor_tensor(out=impS[:, T, :], in0=ob[:, 129:161], scalar=rv,
                                                                         in1=impS[:, T, :], op0=ALU.mult, op1=ALU.add),
                                 reads=[ob.tk, rvk, impS.tk], writes=[impS.tk])

                def mk_final(h, gi):
                    def on_final(qs, ob):
                        T = 4 * c + qs
                        rv, rvk = sm()
                        S.op("dve", lambda e: e.reciprocal(rv, ob[:, 128:129]), reads=[ob.tk], writes=[rvk])
                        cf, cfk = sm()
                        S.op("dve", lambda e: e.tensor_tensor(cf, rv, ng[:, T, 3 * h + gi:3 * h + gi + 1], ALU.mult),
                             reads=[rvk, ng.tk], writes=[cfk])
                        S.op("dve", lambda e: e.scalar_tensor_tensor(out=acc[:, qs, h, :], in0=ob[:, 0:128], scalar=cf,
                                                                     in1=acc[:, qs, h, :], op0=ALU.mult, op1=ALU.add),
                             reads=[ob.tk, cfk, acc.tk], writes=[acc.tk])
                    return on_final

                def win_head(h):
                    def qk_win(kt, q0, q1):
                        return [(kwT[:, kt * 128:(kt + 1) * 128], nqT[:, h, q0:q1], [kwT.tk, nqT.tk])]
                    attention(c, range(max(0, 4 * c - 4), 4 * c + 4), qk_win, lambda kt: vw[:, kt, :], [vw.tk], "win",
                              sbanks, pbufs, obanks, scale, mk_final(h, 2))

                for h in range(4):
                    cmp_head(h)
                    if h > 0:
                        win_head(h - 1)
                biasT = None
                bbs = []
                if c >= 2:
                    biasT = biasT_rot.next()
                    for qs in range(4):
                        T = 4 * c + qs
                        sc = scr.next()
                        S.op("dve", lambda e: e.tensor_tensor(sc[:, :], impS[:, T, :], ntab[:, T, 0, :], ALU.mult),
                             reads=[impS.tk, ntab.tk], writes=[sc.tk])
                        S.op("dve", lambda e: e.tensor_tensor(sc[:, :], sc[:, :], ntab[:, T, 1, :], ALU.add),
                             reads=[sc.tk, ntab.tk], writes=[sc.tk])
                        m8 = mx.next()
                        S.op("dve", lambda e: e.max(out=m8[:, 0:8], in_=sc[:, :]), reads=[sc.tk], writes=[m8.tk])
                        sc2 = scr2.next()
                        S.op("dve", lambda e: e.match_replace(out=sc2[:, :], in_to_replace=m8[:, 0:8], in_values=sc[:, :],
                                                              imm_value=-1e30), reads=[sc.tk, m8.tk], writes=[sc2.tk])
                        S.op("dve", lambda e: e.max(out=m8[:, 8:16], in_=sc2[:, :]), reads=[sc2.tk], writes=[m8.tk])
                        S.op("dve", lambda e: e.tensor_scalar(sc2[:, :], sc[:, :], m8[:, 15:16], None, op0=ALU.is_ge),
                             reads=[sc.tk, m8.tk], writes=[sc2.tk])
                        bb = bsb[qs]
                        S.op("dve", lambda e: e.tensor_scalar(bb[:, :], sc2[:, :], -1.0, -NEG, op0=ALU.add, op1=ALU.mult),
                             reads=[sc2.tk], writes=[bb.tk])
                        bbs.append(bb)
                win_head(3)
                if c >= 2:
                    pt = ptrot.next()
                    for qs in range(4):
                        tp(pt.pap(32, qs * 128, (qs + 1) * 128), bbs[qs][:, :], [bbs[qs].tk], [pt.tk])
                    S.op("dve", lambda e: e.tensor_copy(biasT[:, :], pt.pap(32, 0, 512)), reads=[pt.tk], writes=[biasT.tk])
                for h in range(4):
                    precast(1)

                    def qk_sel(kt, q0, q1):
                        r = [(ksT[:, kt * 128:(kt + 1) * 128], nqT[:, h, q0:q1], [ksT.tk, nqT.tk])]
                        if biasT is not None:
                            r.append((ex[:, kt * 128:(kt + 1) * 128], biasT[:, q0 - c * 512:q1 - c * 512],
                                      [ex.tk, biasT.tk]))
                        return r

                    attention(c, range(0, 4 * c + 4), qk_sel, lambda kt: vs[:, kt, :], [vs.tk], "causal", sbanks, pbufs,
                              obanks, scale, mk_final(h, 1))
                def fin_chunk(acc=acc, c=c):
                    accb = accb_rot.next()
                    act(accb[:, :, :, :], acc[:, :, :, :], AF.Copy, [acc.tk], [accb.tk])
                    for h in range(4):
                        emit_head_out(accb, c, (8 + h) * 128, oTrot, get=lambda qs, h=h: accb[:, qs, h, :])
                deferred_outs.append(fin_chunk)
            flush_outs()
            S.barrier()

    def phase_moba(l):
        with ExitStack() as es:
            mqT = sb(es, "mqT", [128, 4, S_], BF16)
            mkT = sb(es, "mkT", [128, 4, S_], BF16)
            S.dma("sp", mqT[:], PT[17 * 128:21 * 128, :].rearrange("(h p) t -> p h t", p=128), writes=[mqT.tk])
            S.dma("sp", mkT[:], PT[21 * 128:25 * 128, :].rearrange("(h p) t -> p h t", p=128), writes=[mkT.tk])
            mva = sb(es, "mva", [128, 16, 4, 129], BF16)
            for h in range(4):
                S.dma("sp", mva[:, :, h, 0:128], VT[:, 256 + h * 128:256 + (h + 1) * 128].rearrange("(t p) d -> p t d", p=128),
                      writes=[mva.tk])
            S.op("pool", lambda e: e.memset(mva[:, :, :, 128:129], 1.0), writes=[mva.tk])
            mtab = sb(es, "mtab", [128, 16, 3, 8], F32)
            S.dma("sp", mtab[:], C["mobtab"], writes=[mtab.tk])
            ex2 = sb(es, "ex2", [8, 1024], BF16)
            S.dma("sp", ex2[:], C["ex2"], writes=[ex2.tk])
            km = sb(es, "km", [128, 8], F32)
            kmh = sb(es, "kmh", [128, 8], BF16)
            kml = sb(es, "kml", [128, 8], BF16)
            kmd = sb(es, "kmd", [128, 8], F32)
            gm_rot = Rot([sb(es, f"gm{i}", [128, 8], F32) for i in range(2)])
            s01_rot = Rot([sb(es, f"s01{i}", [128, 8], F32) for i in range(2)])
            m8_rot = Rot([sb(es, f"mm8{i}", [128, 8], F32) for i in range(2)])
            bb_rot = Rot([sb(es, f"mbb{i}", [128, 8], BF16) for i in range(2)])
            biasT_rot = Rot([sb(es, f"mbiasT{i}", [8, S_], BF16) for i in range(2)])
            pbufs = Rot([sb(es, f"pT{i}", [128, 512], BF16) for i in range(3)])
            onbs = Rot([sb(es, f"onb{i}", [128, 4, 128], BF16) for i in range(2)])
            oTrot = Rot([sb(es, f"oT{i}", [128, 512], BF16) for i in range(2)])
            sbanks = Rot(ps[0:2])
            obanks = ps[2:6]
            scale = 128.0 ** -0.5
            pgb = ps[6]
            ptl = Rot([PTBank(ps[7])])
            kms = Rot([dict(km=sb(es, f"km{i}", [128, 8], F32), kmh=sb(es, f"kmh{i}", [128, 8], BF16),
                            kml=sb(es, f"kml{i}", [128, 8], BF16), kmd=sb(es, f"kmd{i}", [128, 8], F32)) for i in range(2)])

            def gate_items(h):
                KM = kms.next()
                km_, kmh_, kml_, kmd_ = KM["km"], KM["kmh"], KM["kml"], KM["kmd"]
                biasT = biasT_rot.next()
                items = []

                def prep():
                    S.op("dve", lambda e: e.tensor_reduce(out=km_[:, :], in_=mkT[:, h, :].rearrange("p (n k) -> p n k", k=256),
                                                          axis=AX.X, op=ALU.add), reads=[mkT.tk], writes=[km_.tk])
                    S.op("dve", lambda e: e.tensor_scalar(km_[:, :], km_[:, :], 1.0 / 256, None, op0=ALU.mult),
                         reads=[km_.tk], writes=[km_.tk])
                    S.op("dve", lambda e: e.tensor_copy(kmh_[:, :], km_[:, :]), reads=[km_.tk], writes=[kmh_.tk])
                    S.op("dve", lambda e: e.tensor_tensor(kmd_[:, :], km_[:, :], kmh_[:, :], ALU.subtract),
                         reads=[km_.tk, kmh_.tk], writes=[kmd_.tk])
                    S.op("dve", lambda e: e.tensor_copy(kml_[:, :], kmd_[:, :]), reads=[kmd_.tk], writes=[kml_.tk])
                items.append(prep)
                state = {}
                for T in range(16):
                    def part1(T=T):
                        pg = pgb
                        mm(pg[:, 0:8], mqT[:, h, T * 128:(T + 1) * 128], kmh_[:, :], True, False, [mqT.tk, kmh_.tk], [pg.tk])
                        mm(pg[:, 0:8], mqT[:, h, T * 128:(T + 1) * 128], kml_[:, :], False, True, [mqT.tk, kml_.tk], [pg.tk])
                        gm = gm_rot.next()
                        S.op("dve", lambda e: e.tensor_tensor(gm[:, :], pg[:, 0:8], mtab[:, T, 0, :], ALU.mult),
                             reads=[pg.tk, mtab.tk], writes=[gm.tk])
                        S.op("dve", lambda e: e.tensor_tensor(gm[:, :], gm[:, :], mtab[:, T, 1, :], ALU.add),
                             reads=[gm.tk, mtab.tk], writes=[gm.tk])
                        m8 = m8_rot.next()
                        S.op("dve", lambda e: e.max(out=m8[:, :], in_=gm[:, :]), reads=[gm.tk], writes=[m8.tk])
                        s01 = s01_rot.next()
                        S.op("dve", lambda e: e.tensor_scalar(s01[:, :], gm[:, :], m8[:, 2:3], None, op0=ALU.is_ge),
                             reads=[gm.tk, m8.tk], writes=[s01.tk])
                        S.op("dve", lambda e: e.tensor_tensor(s01[:, :], s01[:, :], mtab[:, T, 0, :], ALU.mult),
                             reads=[s01.tk, mtab.tk], writes=[s01.tk])
                        S.op("dve", lambda e: e.tensor_tensor(s01[:, :], s01[:, :], mtab[:, T, 2, :], ALU.add),
                             reads=[s01.tk, mtab.tk], writes=[s01.tk])
                        bb = bb_rot.next()
                        S.op("dve", lambda e: e.tensor_scalar(bb[:, :], s01[:, :], -1.0, -NEG, op0=ALU.add, op1=ALU.mult),
                             reads=[s01.tk], writes=[bb.tk])
                        state[T] = bb

                    def part2(T=T):
                        pt = ptl.next()
                        bb = state[T]
                        tp(pt.pap(8, 0, 128), bb[:, :], [bb.tk], [pt.tk])
                        S.op("dve", lambda e: e.tensor_copy(biasT[:, T * 128:(T + 1) * 128], pt.pap(8, 0, 128)),
                             reads=[pt.tk], writes=[biasT.tk])
                    items.append(part1)
                    items.append(part2)
                p1s, p2s = items[1::2], items[2::2]
                order_ = [items[0], p1s[0]]
                for T in range(1, 16):
                    order_ += [p1s[T], p2s[T - 1]]
                order_.append(p2s[15])
                return biasT, order_

            biasT, its = gate_items(0)
            for f in its:
                f()
            for h in range(4):
                if h + 1 < 4:
                    biasT_next, pending = gate_items(h + 1)
                else:
                    biasT_next, pending = None, []
                pending = list(pending)

                def hook():
                    if pending:
                        pending.pop(0)()

                def qk(kt, q0, q1):
                    nblk = kt // 2
                    return [(mkT[:, h, kt * 128:(kt + 1) * 128], mqT[:, h, q0:q1], [mkT.tk, mqT.tk]),
                            (ex2[:, nblk * 128:(nblk + 1) * 128], biasT[:, q0:q1], [ex2.tk, biasT.tk])]

                for c in range(4):
                    precast(1)
                    onb = onbs.next()

                    def on_final(qs, ob):
                        rv, rvk = sm()
                        S.op("dve", lambda e: e.reciprocal(rv, ob[:, 128:129]), reads=[ob.tk], writes=[rvk])
                        S.op("dve", lambda e: e.tensor_scalar(onb[:, qs, :], ob[:, 0:128], rv, None, op0=ALU.mult),
                             reads=[ob.tk, rvk], writes=[onb.tk])
                        if qs == 3:
                            deferred_outs.append(lambda onb=onb, c=c, h=h: emit_head_out(onb, c, (12 + h) * 128, oTrot, pr=ptl))

                    attention(c, range(0, 4 * c + 4), qk, lambda kt: mva[:, kt, h, :], [mva.tk], "causal", sbanks, pbufs,
                              obanks, scale, on_final, hook=hook)
                while pending:
                    pending.pop(0)()
                biasT = biasT_next
            flush_outs()
            precast(1000)
            S.barrier()

    def phase_proj_ln(XT_d, W_ap, hin_d, g_ap, b_ap, hout_d, hTout_d):
        with ExitStack() as es:
            Wb = sb(es, "Wb", [128, 16, D], BF16)
            Wtk = [Tk() for _ in range(4)]
            for c4 in range(4):
                S.dma("pool", Wb[:, :, c4 * 512:(c4 + 1) * 512],
                      W_ap[:, c4 * 512:(c4 + 1) * 512].rearrange("(kt p) n -> p kt n", p=128), writes=[Wtk[c4]])
            L = ln_setup(es, g_ap, b_ap, slim=True)
            xrot = Rot([sb(es, f"xTt{i}", [128, 16, 512], BF16) for i in range(2)])

            def load_x(tb):
                xb = xrot.next()
                S.dma("sp", xb[:], XT_d[:, tb * 512:(tb + 1) * 512].rearrange("(kt p) q -> p kt q", p=128), writes=[xb.tk])
                return xb

            nx = load_x(0)
            nh = ln_prefetch(L, hin_d, 0)
            pend = None
            xb = None
            for t in range(NT):
                if t % 4 == 0:
                    xb = nx
                    if t + 4 < NT:
                        nx = load_x(t // 4 + 1)
                hb_ = nh
                if t + 1 < NT:
                    nh = ln_prefetch(L, hin_d, t + 1)
                tq = t % 4
                for c4 in range(4):
                    for kt in range(16):
                        mm(ps[c4][:, :], xb[:, kt, tq * 128:(tq + 1) * 128], Wb[:, kt, c4 * 512:(c4 + 1) * 512], kt == 0, kt == 15,
                           [xb.tk, Wtk[c4]], [ps[c4].tk])
                prev = pend
                pend = ln_tile(L, t, hb_, ps[0:4], hout_d, hTout_d)
                if prev is not None:
                    prev()
            pend()
            S.barrier()

    def phase_memT():
        with ExitStack() as es:
            mf = sb(es, "memf", [128, 2, D], F32)
            S.dma("sp", mf[:], mem_d.rearrange("(mt p) d -> p mt d", p=128), writes=[mf.tk])
            mb = sb(es, "memb", [128, 2, D], BF16)
            act(mb[:], mf[:], AF.Copy, [mf.tk], [mb.tk])
            mT = sb(es, "memT", [128, 16, 256], BF16)
            for mt in range(2):
                for g2 in range(2):
                    pt = ptrot.next()
                    for j in range(8):
                        kt = g2 * 8 + j
                        tp(pt.ap(j * 128, (j + 1) * 128), mb[:, mt, kt * 128:(kt + 1) * 128], [mb.tk], [pt.tk])
                    S.op("dve", lambda e: e.tensor_copy(mT[:, g2 * 8:(g2 + 1) * 8, mt * 128:(mt + 1) * 128],
                                                        pt.ap(0, 1024).rearrange("p (j q) -> p j q", j=8)),
                         reads=[pt.tk], writes=[mT.tk])
            S.dma("sp", MEMT.rearrange("(kt p) m -> p kt m", p=128), mT[:], reads=[mT.tk])
            S.barrier()

    def phase_cross(l, hT_src):
        with ExitStack() as es:
            hT = sb(es, "hT", [128, 16, S_], BF16)
            hT_tk = [Tk() for _ in range(4)]
            for g in range(4):
                S.dma("sp", hT[:, :, g * 512:(g + 1) * 512],
                      hT_src[:, g * 512:(g + 1) * 512].rearrange("(kt p) t -> p kt t", p=128), writes=[hT_tk[g]])
            mT = sb(es, "memT", [128, 16, 256], BF16)
            S.dma("sp", mT[:], MEMT.rearrange("(kt p) m -> p kt m", p=128), writes=[mT.tk])
            wrot = Rot([sb(es, f"wsl{i}", [128, 16, 512], BF16) for i in range(2)])
            kxT = sb(es, "kxT", [128, 16, 256], BF16)
            vx = sb(es, "vx", [128, 2, D], BF16)
            ones = sb(es, "ones", [128, 2], BF16)
            S.op("pool", lambda e: e.memset(ones[:, :], 1.0), writes=[ones.tk])
            qx_rot = Rot([sb(es, f"qx{i}", [128, 4, 512], BF16) for i in range(2)])
            pT_rot = Rot([sb(es, f"pTx{i}", [128, 2, 512], BF16) for i in range(2)])
            ox_rot = Rot([sb(es, f"ox{i}", [128, 4, 512], BF16) for i in range(2)])
            oTrot = Rot([sb(es, f"oT{i}", [128, 512], BF16) for i in range(2)])
            sbanks = Rot(ps[0:2])
            obanks = ps[2:6]
            sumb = ps[6]
            ptx = PTBank(ps[7])
            scale = 512.0 ** -0.5

            def load_slab(Wap, c0):
                wb = wrot.next()
                S.dma("pool", wb[:, :, :], Wap[:, c0:c0 + 512].rearrange("(kt p) n -> p kt n", p=128), writes=[wb.tk])
                return wb

            wkv = WKVB
            wq = WQB
            nxt = load_slab(wkv, 0)
            for sl in range(8):
                wb = nxt
                if sl + 1 < 8:
                    nxt = load_slab(wkv, (sl + 1) * 512)
                else:
                    nxt = load_slab(wq, 0)
                if sl < 4:
                    for j in range(4):
                        pb = sbanks.next()
                        for kt in range(16):
                            mm(pb[:, 0:256], wb[:, kt, j * 128:(j + 1) * 128], mT[:, kt, :], kt == 0, kt == 15,
                               [wb.tk, mT.tk], [pb.tk])
                        act(kxT[:, sl * 4 + j, :], pb[:, 0:256], AF.Copy, [pb.tk], [kxT.tk])
                else:
                    for mt in range(2):
                        pb = sbanks.next()
                        for kt in range(16):
                            mm(pb[:, :], mT[:, kt, mt * 128:(mt + 1) * 128], wb[:, kt, :], kt == 0, kt == 15,
                               [wb.tk, mT.tk], [pb.tk])
                        act(vx[:, mt, (sl - 4) * 512:(sl - 3) * 512], pb[:, :], AF.Copy, [pb.tk], [vx.tk])
            def qproj(wb, c):
                cs = slice(c * 512, (c + 1) * 512)
                qx = qx_rot.next()
                for j in range(4):
                    pb = sbanks.next()
                    for kt in range(16):
                        mm(pb[:, :], wb[:, kt, j * 128:(j + 1) * 128], hT[:, kt, cs], kt == 0, kt == 15,
                           [wb.tk, hT_tk[c]], [pb.tk])
                    act(qx[:, j, :], pb[:, :], AF.Copy, [pb.tk], [qx.tk])
                return qx

            def xattn(h, c, qx):
                cs = slice(c * 512, (c + 1) * 512)
                pT = pT_rot.next()
                for mt in range(2):
                    sp = sbanks.next()
                    for j in range(4):
                        mm(sp[:, :], kxT[:, h * 4 + j, mt * 128:(mt + 1) * 128], qx[:, j, :], j == 0, j == 3,
                           [kxT.tk, qx.tk], [sp.tk])
                    act(pT[:, mt, :], sp[:, :], AF.Exp, [sp.tk], [pT.tk], scale=float(scale))
                ox = ox_rot.next()
                for qs in range(4):
                    ob = obanks[qs]
                    for mt in range(2):
                        mm(ob[:, :], pT[:, mt, qs * 128:(qs + 1) * 128], vx[:, mt, h * 512:(h + 1) * 512], mt == 0, mt == 1,
                           [pT.tk, vx.tk], [ob.tk])
                    for mt in range(2):
                        mm(sumb[:, qs:qs + 1], pT[:, mt, qs * 128:(qs + 1) * 128], ones[:, 0:1], mt == 0, mt == 1,
                           [pT.tk, ones.tk], [sumb.tk])
                    rv, rvk = sm()
                    S.op("dve", lambda e: e.reciprocal(rv, sumb[:, qs:qs + 1]), reads=[sumb.tk], writes=[rvk])
                    act(ox[:, qs, :], ob[:, :], AF.Copy, [ob.tk, rvk], [ox.tk], scale=rv)

                def outs():
                    for j in range(4):
                        for qs in range(4):
                            tp(ptx.ap(qs * 128, (qs + 1) * 128), ox[:, qs, j * 128:(j + 1) * 128], [ox.tk], [ptx.tk])
                        oT = oTrot.next()
                        S.op("dve", lambda e: e.tensor_copy(oT[:, :], ptx.ap(0, 512)), reads=[ptx.tk], writes=[oT.tk])
                        r0 = (h * 4 + j) * 128
                        S.dma("sp", XOT[r0:r0 + 128, cs], oT[:, :], reads=[oT.tk])
                return outs

            steps = [(h, c) for h in range(4) for c in range(4)]
            slabs = {0: nxt}
            qx_cur = qproj(slabs[0], 0)
            pend_out = None
            for i, (h, c) in enumerate(steps):
                qx_next = None
                if c == 0 and h + 1 < 4:
                    slabs[h + 1] = load_slab(wq, (h + 1) * 512)
                if i + 1 < len(steps):
                    h2, c2 = steps[i + 1]
                    qx_next = qproj(slabs[h2], c2)
                if pend_out is not None:
                    pend_out()
                pend_out = xattn(h, c, qx_cur)
                qx_cur = qx_next
            pend_out()
            S.barrier()

    def phase_mlp(l, hT_src, hin_d, g_ap, b_ap, hout_d, hTout_d):
        with ExitStack() as es:
            L = LNCtx()
            L.store_q = "sp"
            L.cast_eng = "act"
            L.gB = sb(es, "ln_gB", [128, D], F32)
            L.bB = sb(es, "ln_bB", [128, D], F32)
            S.dma("sp", L.gB[:], g_ap.partition_broadcast(128), writes=[L.gB.tk])
            S.dma("sp", L.bB[:], b_ap.partition_broadcast(128), writes=[L.bB.tk])
            hbj = sb(es, "ln_hbj", [128, D], BF16)
            L.junk = hbj
            L.hb = Rot([hbj])
            L.hTt = None
            hbufs = Rot([sb(es, f"hbuf{i}", [128, 16, 512], BF16) for i in range(2)])
            HT = sb(es, "HT", [128, 64, 512], BF16)
            HTk = [Tk() for _ in range(64)]
            w1rot = Rot([sb(es, f"w1s{i}", [128, 16, 256], BF16) for i in range(2)])
            w2rot = Rot([sb(es, f"w2s{i}", [128, 16, 512], BF16) for i in range(2)])
            y4 = sb(es, "y4", [128, 4, D], F32)
            y4k = [Tk() for _ in range(4)]
            rrot = Rot([sb(es, f"rr{i}", [128, 512], BF16) for i in range(4)])
            w1 = W1B
            w2 = W2B
            prot = Rot(ps[0:6])

            def load_w1(fs):
                wb = w1rot.next()
                S.dma("pool", wb[:], w1[:, fs * 256:(fs + 1) * 256].rearrange("(kt p) n -> p kt n", p=128), writes=[wb.tk])
                return wb

            def load_w2(i):
                c4, qd = i // 4, i % 4
                wb = w2rot.next()
                S.dma("pool", wb[:], w2[qd * 2048:(qd + 1) * 2048, c4 * 512:(c4 + 1) * 512].rearrange("(ft p) n -> p ft n", p=128),
                      writes=[wb.tk])
                return wb

            pend_ln = None
            for tb in range(4):
                t0 = tb * 512
                hbuf = hbufs.next()
                S.dma("sp", hbuf[:], hT_src[:, t0:t0 + 512].rearrange("(kt p) q -> p kt q", p=128), writes=[hbuf.tk])
                nw = load_w1(0)
                fins = {}
                for fs in range(32):
                    wb = nw
                    if fs + 1 < 32:
                        nw = load_w1(fs + 1)
                    else:
                        nw2 = load_w2(0)
                    for ft in range(2):
                        f = fs * 2 + ft
                        pb = prot.next()
                        for kt in range(16):
                            mm(pb[:, :], wb[:, kt, ft * 128:(ft + 1) * 128], hbuf[:, kt, :], kt == 0, kt == 15,
                               [wb.tk, hbuf.tk], [pb.tk])
                        rr = rrot.next()
                        S.op("dve", lambda e: e.tensor_scalar(rr[:, :], pb[:, :], 0.0, None, op0=ALU.max),
                             reads=[pb.tk], writes=[rr.tk])
                        S.op("dve", lambda e: e.tensor_tensor(HT[:, f, :], rr[:, :], rr[:, :], ALU.mult),
                             reads=[rr.tk], writes=[HTk[f]])
                    if pend_ln is not None:
                        if fs % 7 == 2 and fs // 7 < 4:
                            fins[fs // 7] = pend_ln(fs // 7)
                        if fs % 7 == 6 and fs // 7 < 4:
                            fins[fs // 7]()
                pend_ln = None
                for tt in range(4):
                    S.dma("sp", y4[:, tt, :], hin_d[t0 + tt * 128:t0 + (tt + 1) * 128, :], writes=[y4k[tt]])
                for i in range(16):
                    c4, qd = i // 4, i % 4
                    wb = nw2
                    if i + 1 < 16:
                        nw2 = load_w2(i + 1)
                    pa = ps[0:4] if c4 % 2 == 0 else ps[4:8]
                    for tt in range(4):
                        for ft in range(16):
                            f = qd * 16 + ft
                            mm(pa[tt][:, :], HT[:, f, tt * 128:(tt + 1) * 128], wb[:, ft, :], qd == 0 and ft == 0,
                               qd == 3 and ft == 15, [HTk[f], wb.tk], [pa[tt].tk])
                    if qd == 3:
                        for tt in range(4):
                            S.op("dve", lambda e: e.scalar_tensor_tensor(
                                out=y4[:, tt, c4 * 512:(c4 + 1) * 512], in0=y4[:, tt, c4 * 512:(c4 + 1) * 512],
                                scalar=float(ALPHA), in1=pa[tt][:, :], op0=ALU.mult, op1=ALU.add),
                                reads=[y4k[tt], pa[tt].tk], writes=[y4k[tt]])
                def mk_ln(tb=tb, hbuf=hbuf):
                    def f(tt):
                        L.cur_hTt = hbuf
                        return ln_core(L, tb * 4 + tt, y4[:, tt, :], y4k[tt], y4[:, tt, :], y4k[tt], hout_d, hTout_d)
                    return f
                pend_ln = mk_ln()
            for tt in range(4):
                pend_ln(tt)()
            S.barrier()

    def finish():
        S.barrier()
        es_glob.close()
        return nc

    order = []
    order.append(("ln_in", phase_ln_in))
    order.append(("memT", phase_memT))
    for l in range(depth):
        last = (l == depth - 1)
        a = l % 2
        b = 1 - a
        order.append((f"proj{l}", lambda l=l, a=a: phase_proj(l, hTd[a])))
        order.append((f"mla{l}", lambda l=l: phase_mla(l)))
        order.append((f"nsa{l}", lambda l=l: phase_nsa(l)))
        order.append((f"moba{l}", lambda l=l: phase_moba(l)))
        order.append((f"out{l}", lambda l=l, a=a, b=b: phase_proj_ln(CATT, WOUTB, hres[a], W["ln1_g"][l],
                                                                    W["ln1_b"][l], hres[b], hTd[b])))
        order.append((f"cross{l}", lambda l=l, b=b: phase_cross(l, hTd[b])))
        order.append((f"wo{l}", lambda l=l, a=a, b=b: phase_proj_ln(XOT, WOB, hres[b], W["ln2_g"][l],
                                                                   W["ln2_b"][l], hres[a], hTd[a])))
        if last:
            order.append((f"mlp{l}", lambda l=l, a=a: phase_mlp(l, hTd[a], hres[a], W["ln3_g"][l], W["ln3_b"][l],
                                                               out_d, None)))
        else:
            order.append((f"mlp{l}", lambda l=l, a=a, b=b: phase_mlp(l, hTd[a], hres[a], W["ln3_g"][l], W["ln3_b"][l],
                                                                    hres[b], hTd[b])))
    skip = set()
    if isinstance(stop_after, (tuple, list)):
        skip = set(stop_after[1])
        stop_after = stop_after[0]
    for name, fn in order:
        if name not in skip:
            fn()
        PHASE_LOG.append((name, S.pcnt["pe"] + 30000 * (S.nsem_pe_rot if hasattr(S, "nsem_pe_rot") else 0), S.ninst))
        if name == stop_after:
            break
    return finish()


_CACHE = {}


def kernel(**inputs):
    if "nc" not in _CACHE:
        _CACHE["nc"] = build_program()
        _CACHE["consts"] = make_consts()
    nc = _CACHE["nc"]
    consts = _CACHE["consts"]
    x = np.ascontiguousarray(np.asarray(inputs["x"], dtype=np.float32))
    mem = np.ascontiguousarray(np.asarray(inputs["mem"], dtype=np.float32))
    shared = {n: np.ascontiguousarray(np.asarray(inputs[n], dtype=np.float32)) for n, _ in W_SPECS}
    shared.update(consts)
    in_maps = []
    for b in range(8):
        m = dict(shared)
        m["x"] = x[b]
        m["mem"] = mem[b]
        in_maps.append(m)
    res = run_bass_kernel_spmd(nc, in_maps, core_ids=list(range(8)))
    return np.stack([np.asarray(r["out"], dtype=np.float32) for r in res.results], axis=0)
```

```python
from contextlib import ExitStack

import ml_dtypes
import numpy as np

import concourse.bass as bass
import concourse.mybir as mybir
from concourse.bass_utils import run_bass_kernel_spmd

F32 = mybir.dt.float32
BF16 = mybir.dt.bfloat16
AF = mybir.ActivationFunctionType
ALU = mybir.AluOpType
AX = mybir.AxisListType

D = 2048
S_ = 2048
NT = 16
DEPTH = 2
INW = 3916
DFF = 8192
ALPHA = (2 * DEPTH) ** 0.25
NEG = -30000.0
JOBLIM = None
PHASE_LOG = []


class Tk:
    __slots__ = ("w", "r", "excl")

    def __init__(self):
        self.w = None
        self.r = {}
        self.excl = False


class Buf:
    def __init__(self, h):
        self.h = h
        self.tk = Tk()

    def __getitem__(self, k):
        return self.h[k]


class Sched:
    def __init__(self, nc):
        self.nc = nc
        self.e = {"pe": nc.tensor, "act": nc.scalar, "dve": nc.vector, "pool": nc.gpsimd, "sp": nc.sync}
        self.nsem = 0
        self.psem = {k: self._sem() for k in self.e}
        self.pcnt = {k: 0 for k in self.e}
        self.last = {k: None for k in self.e}
        self.waited = {k: {} for k in self.e}
        self.dsem = {q: [[self._sem(), 0] for _ in range(n)] for q, n in (("sp", 16), ("pool", 16), ("act", 4))}
        self.drr = {q: 0 for q in self.dsem}
        self.ninst = 0

    def _sem(self):
        self.nsem += 1
        return self.nc.semaphore(f"sm{self.nsem}").__enter__()

    def need(self, eng, tok, same_ok):
        if tok is None:
            return
        sem, val, src = tok
        if src == eng and same_ok:
            return
        w = self.waited[eng]
        if w.get(sem.num, 0) >= val:
            return
        self.e[eng].wait_ge(sem, val)
        w[sem.num] = val

    def op(self, eng, fn, reads=(), writes=()):
        for t in reads:
            self.need(eng, t.w, eng == "pe")
            if t.excl:
                for tok in t.r.values():
                    self.need(eng, tok, True)
        for t in writes:
            self.need(eng, t.w, eng == "pe")
            for tok in t.r.values():
                self.need(eng, tok, eng == "pe")
        ins = fn(self.e[eng])
        if self.pcnt[eng] >= 30000:
            self.psem[eng] = self._sem()
            self.pcnt[eng] = 0
        self.pcnt[eng] += 1
        sem = self.psem[eng]
        ins.then_inc(sem, 1)
        tok = (sem, self.pcnt[eng], eng)
        self.last[eng] = tok
        for t in reads:
            t.r[eng] = tok
        for t in writes:
            t.w = tok
            t.r = {}
        self.ninst += 1
        return tok

    def dma(self, q, out, in_, reads=(), writes=()):
        for t in reads:
            self.need(q, t.w, False)
        for t in writes:
            self.need(q, t.w, False)
            for tok in t.r.values():
                self.need(q, tok, False)
        lst = self.dsem[q]
        i = self.drr[q]
        self.drr[q] = (i + 1) % len(lst)
        ent = lst[i]
        if ent[1] > 0:
            self.need(q, (ent[0], 16 * ent[1], "dma"), False)
        ent[1] += 1
        self.e[q].dma_start(out=out, in_=in_).then_inc(ent[0], 16)
        tok = (ent[0], 16 * ent[1], "dma")
        for t in reads:
            t.r[("d", ent[0].num)] = tok
        for t in writes:
            t.w = tok
            t.r = {}
        self.ninst += 1
        return tok

    def barrier(self):
        toks = [t for t in self.last.values() if t is not None]
        for q in self.dsem:
            for ent in self.dsem[q]:
                if ent[1] > 0:
                    toks.append((ent[0], 16 * ent[1], "dma"))
        for eng in self.e:
            for tok in toks:
                self.need(eng, tok, True)


class Rot:
    def __init__(self, bufs):
        self.bufs = bufs
        self.i = 0

    def next(self):
        b = self.bufs[self.i]
        self.i = (self.i + 1) % len(self.bufs)
        return b


def _rope_tab(dim):
    inv = (np.float32(500000.0) ** (-np.arange(0, dim, 2, dtype=np.float32) / np.float32(dim))).astype(np.float32)
    ang = (np.arange(S_, dtype=np.float32)[:, None] * inv[None, :]).astype(np.float32)
    cos = np.cos(ang).astype(np.float32).T
    sin = np.sin(ang).astype(np.float32).T
    C = np.concatenate([cos, cos], 0)
    Sn = np.concatenate([sin, sin], 0)
    half = dim // 2
    P = np.zeros((dim, dim), np.float32)
    for m in range(half):
        P[m + half, m] = -1.0
    for m in range(half, dim):
        P[m - half, m] = 1.0
    return np.ascontiguousarray(C), np.ascontiguousarray(Sn), P


def make_consts():
    bf = ml_dtypes.bfloat16
    c = {}
    c["ident"] = np.eye(128, dtype=np.float32).astype(bf)
    k = np.arange(128)[:, None]
    q = np.arange(128)[None, :]
    c["tri"] = np.where(k <= q, 0.0, NEG).astype(np.float32).astype(bf)
    c["anti"] = np.where(k > q, 0.0, NEG).astype(np.float32).astype(bf)
    C64, S64, P64 = _rope_tab(64)
    C32, S32, P32 = _rope_tab(32)
    c["rope64"] = np.stack([C64, S64], 1).astype(np.float32)
    c["rope32"] = np.stack([C32, S32], 1).astype(np.float32)
    c["p64"] = P64.astype(bf)
    c["p32"] = P32.astype(bf)
    t = np.arange(S_)
    n = np.arange(128)
    cm = ((16 * n[:, None] + 31) <= t[None, :]) & (n[:, None] < 127)
    c["cmpmask"] = np.where(cm, 0.0, NEG).astype(np.float32).astype(bf)
    starts = np.arange(127) * 16
    sel_start = np.arange(32) * 64
    ov = ((starts[:, None] < sel_start[None, :] + 64) & (starts[:, None] + 32 > sel_start[None, :])).astype(np.float32)
    ovf = np.zeros((128, 33), np.float32)
    ovf[:, 0] = 1.0
    ovf[:127, 1:] = ov
    c["ov"] = ovf.astype(bf)
    cur = t // 64
    j = np.arange(32)
    elig = j[None, :] <= cur[:, None]
    f0 = (j[None, :] == 0)
    f1 = (j[None, :] == cur[:, None])
    f2 = (j[None, :] == cur[:, None] - 1)
    forced = f0 | f1 | f2
    E = (elig & ~forced).astype(np.float32)
    Fv = np.where(~elig, -1e30, 0.0) + np.where(elig & f2, 1e4, 0.0) + np.where(elig & f1, 2e4, 0.0) + np.where(elig & f0, 4e4, 0.0)
    nsa = np.stack([E, Fv.astype(np.float32)], 1)
    c["nsatab"] = np.ascontiguousarray(nsa.reshape(16, 128, 2, 32).transpose(1, 0, 2, 3)).astype(np.float32)
    kk = np.arange(S_)
    c["ex"] = ((kk[None, :] // 64) == j[:, None]).astype(np.float32).astype(bf)
    curm = t // 256
    nb = np.arange(8)
    E2 = (nb[None, :] < curm[:, None]).astype(np.float32)
    F2 = np.where(nb[None, :] < curm[:, None], 0.0, -1e30).astype(np.float32)
    OWN = (nb[None, :] == curm[:, None]).astype(np.float32)
    mob = np.stack([E2, F2, OWN], 1)
    c["mobtab"] = np.ascontiguousarray(mob.reshape(16, 128, 3, 8).transpose(1, 0, 2, 3)).astype(np.float32)
    ex2 = np.zeros((8, 1024), np.float32)
    for b in range(8):
        ex2[b, b * 128:(b + 1) * 128] = 1.0
    c["ex2"] = ex2.astype(bf)
    return c


CONST_SPECS = [
    ("ident", [128, 128], BF16), ("tri", [128, 128], BF16), ("anti", [128, 128], BF16),
    ("rope64", [64, 2, S_], F32), ("rope32", [32, 2, S_], F32), ("p64", [64, 64], BF16), ("p32", [32, 32], BF16),
    ("cmpmask", [128, S_], BF16), ("ov", [128, 33], BF16), ("nsatab", [128, 16, 2, 32], F32), ("ex", [32, S_], BF16),
    ("mobtab", [128, 16, 3, 8], F32), ("ex2", [8, 1024], BF16),
]

W_SPECS = [
    ("ln_in_g", [D]), ("ln_in_b", [D]), ("w_in", [DEPTH, D, INW]), ("mla_q_norm", [DEPTH, 512]),
    ("mla_kv_norm", [DEPTH, 512]), ("mla_w_uq", [DEPTH, 512, 1536]), ("mla_w_ukv", [DEPTH, 512, 2048]),
    ("nsa_cmp_w1", [DEPTH, 2, 4096, 128]), ("nsa_cmp_w2", [DEPTH, 2, 128, 128]), ("nsa_cmp_pos", [DEPTH, 2, 32, 128]),
    ("w_out", [DEPTH, D, D]), ("ln1_g", [DEPTH, D]), ("ln1_b", [DEPTH, D]), ("mem_wq", [DEPTH, D, D]),
    ("mem_wkv", [DEPTH, D, 2 * D]), ("mem_wo", [DEPTH, D, D]), ("ln2_g", [DEPTH, D]), ("ln2_b", [DEPTH, D]),
    ("mlp_w1", [DEPTH, D, DFF]), ("mlp_w2", [DEPTH, DFF, D]), ("ln3_g", [DEPTH, D]), ("ln3_b", [DEPTH, D]),
]


def build_program(dbg=False, stop_after=None, depth=DEPTH):
    nc = bass.Bass("TRN2", target_bir_lowering=False)
    S = Sched(nc)
    skind = "ExternalOutput" if dbg else "Internal"

    def din(name, shape, dt=F32):
        return nc.dram_tensor(name, list(shape), dt, kind="ExternalInput").ap()

    def dscr(name, shape, dt):
        return nc.dram_tensor(name, list(shape), dt, kind=skind).ap()

    x_d = din("x", [S_, D])
    mem_d = din("mem", [256, D])
    W = {n: din(n, s) for n, s in W_SPECS}
    C = {n: din(n, s, dt) for n, s, dt in CONST_SPECS}
    out_d = nc.dram_tensor("out", [S_, D], F32, kind="ExternalOutput").ap()

    hres = [dscr(f"hres{i}", [S_, D], F32) for i in range(2)]
    hTd = [dscr(f"hT{i}", [D, S_], BF16) for i in range(2)]
    PT = dscr("PT", [25 * 128, S_], BF16)
    VT = dscr("VT", [S_, 768], BF16)
    NG = dscr("NG", [S_, 12], F32)
    CATT = dscr("CATT", [D, S_], BF16)
    XOT = dscr("XOT", [D, S_], BF16)
    MEMT = dscr("MEMT", [D, 256], BF16)
    WOUTB = nc.dram_tensor("WOUTB", [D, D], BF16).ap()
    WQB = nc.dram_tensor("WQB", [D, D], BF16).ap()
    WKVB = nc.dram_tensor("WKVB", [D, 2 * D], BF16).ap()
    WOB = nc.dram_tensor("WOB", [D, D], BF16).ap()
    W1B = nc.dram_tensor("W1B", [D, DFF], BF16).ap()
    W2B = nc.dram_tensor("W2B", [DFF, D], BF16).ap()
    precast_q = []

    def precast_plan(l):
        jobs = []
        for dst, src, rows in ((WOUTB, W["w_out"][l], 512), (WKVB, W["mem_wkv"][l], 256), (WQB, W["mem_wq"][l], 512),
                               (WOB, W["mem_wo"][l], 512), (W1B, W["mlp_w1"][l], 128), (W2B, W["mlp_w2"][l], 512)):
            n = dst.shape[0]
            for r0 in range(0, n, rows):
                jobs.append((dst[r0:r0 + rows, :], src[r0:r0 + rows, :]))
        precast_q.extend(jobs)

    def precast(n):
        for _ in range(n):
            if precast_q:
                d_, s_ = precast_q.pop(0)
                S.dma("pool", d_, s_)

    es_glob = ExitStack()

    uid = [0]

    def sb(es, name, shape, dt):
        uid[0] += 1
        return Buf(es.enter_context(nc.sbuf_tensor(f"sb{uid[0]}_{name}", list(shape), dt)))

    ps = [Buf(es_glob.enter_context(nc.psum_tensor(f"ps{i}", [128, 512], F32))) for i in range(8)]
    for _b in ps:
        _b.tk.excl = True

    class PTBank:
        def __init__(self, b):
            self.b = b
            self.tk = b.tk
            self.v = b[:, :].bitcast(BF16)

        def ap(self, a, b):
            return self.v[:, a:b]

        def pap(self, p, a, b):
            return self.v[0:p, a:b]

    ptrot = Rot([PTBank(ps[6]), PTBank(ps[7])])
    ptrot_raw = Rot([ps[6], ps[7]])

    ident = sb(es_glob, "ident", [128, 128], BF16)
    tri = sb(es_glob, "tri", [128, 128], BF16)
    anti = sb(es_glob, "anti", [128, 128], BF16)
    S.dma("sp", ident[:], C["ident"], writes=[ident.tk])
    S.dma("sp", tri[:], C["tri"], writes=[tri.tk])
    S.dma("sp", anti[:], C["anti"], writes=[anti.tk])
    small = sb(es_glob, "small", [128, 64], F32)
    small_tk = [Tk() for _ in range(64)]
    small_i = [0]

    acc0 = sb(es_glob, "acc0", [128, 512], F32)
    acc0_tok = S.op("dve", lambda e: e.memset(acc0[:, :], 0.0), writes=[acc0.tk])
    acc_i = [0]

    def zcol():
        i = acc_i[0]
        acc_i[0] = i + 1
        assert i < 512
        tk = Tk()
        tk.w = acc0_tok
        return acc0[:, i:i + 1], tk

    def sm():
        i = small_i[0]
        small_i[0] = (i + 1) % 64
        return small[:, i:i + 1], small_tk[i]

    def mm(out, lhsT, rhs, start, stop, reads, writes):
        return S.op("pe", lambda e: e.matmul(out, lhsT, rhs, start=start, stop=stop), reads=reads, writes=writes)

    def tp(out, in_, reads, writes, kp=128):
        return S.op("pe", lambda e: e.transpose(out, in_, ident[0:kp, 0:kp]), reads=list(reads) + [ident.tk], writes=writes)

    def act(out, in_, func, reads, writes, **kw):
        return S.op("act", lambda e: e.activation(out, in_, func, **kw), reads=reads, writes=writes)

    class LNCtx:
        pass

    class _B2:
        def __init__(self, ap, tk):
            self.ap_ = ap
            self.tk = tk

        def __getitem__(self, k):
            if k == slice(None):
                return self.ap_
            return self.ap_[k]

    def ln_setup(es, g_ap, b_ap, slim=False):
        L = LNCtx()
        L.store_q = "sp"
        L.cast_eng = "act"
        nb = 2
        L.gB = sb(es, "ln_gB", [128, D], F32)
        L.bB = sb(es, "ln_bB", [128, D], F32)
        S.dma("sp", L.gB[:], g_ap.partition_broadcast(128), writes=[L.gB.tk])
        S.dma("sp", L.bB[:], b_ap.partition_broadcast(128), writes=[L.bB.tk])
        L.hin = Rot([sb(es, f"ln_hin{i}", [128, D], F32) for i in range(2)])
        L.y = Rot([sb(es, f"ln_y{i}", [128, D], F32) for i in range(nb)])
        L.junk = sb(es, "ln_junk", [128, D], BF16)
        L.hb = Rot([sb(es, f"ln_hb{i}", [128, D], BF16) for i in range(2)])
        L.hTt = Rot([sb(es, f"ln_hTt{i}", [128, 16, 512], BF16) for i in range(2)])
        L.cur_hTt = None
        return L

    def ln_prefetch(L, hin_d, t):
        hb = L.hin.next()
        S.dma("sp", hb[:], hin_d[t * 128:(t + 1) * 128, :], writes=[hb.tk])
        return hb

    def ln_tile(L, t, hin_buf, psb, hout_d, hTout_d):
        y = L.y.next()
        if psb is None:
            ysrc = hin_buf
        else:
            for c4 in range(4):
                S.op("dve", lambda e: e.scalar_tensor_tensor(
                    out=y[:, c4 * 512:(c4 + 1) * 512], in0=hin_buf[:, c4 * 512:(c4 + 1) * 512], scalar=float(ALPHA),
                    in1=psb[c4][:, :], op0=ALU.mult, op1=ALU.add),
                    reads=[hin_buf.tk, psb[c4].tk], writes=[y.tk])
            ysrc = y
        return ln_core(L, t, ysrc[:], ysrc.tk, y[:], y.tk, hout_d, hTout_d)

    def ln_core(L, t, ysrc_ap, ysrc_tk, y_ap, y_tk, hout_d, hTout_d):
        y = _B2(y_ap, y_tk)
        ysrc = _B2(ysrc_ap, ysrc_tk)
        s1, s1k = zcol()
        s2, s2k = zcol()
        act(L.junk[:], ysrc[:], AF.Copy, [ysrc.tk, s1k], [L.junk.tk, s1k], accum_out=s1)
        act(L.junk[:], ysrc[:], AF.Square, [ysrc.tk, s2k], [L.junk.tk, s2k], accum_out=s2)
        mean, meank = sm()
        S.op("dve", lambda e: e.tensor_scalar(mean, s1, 1.0 / D, None, op0=ALU.mult), reads=[s1k], writes=[meank])
        msq, msqk = sm()
        S.op("dve", lambda e: e.tensor_tensor(msq, mean, mean, ALU.mult), reads=[meank], writes=[msqk])
        var, vark = sm()
        S.op("dve", lambda e: e.scalar_tensor_tensor(out=var, in0=s2, scalar=1.0 / D, in1=msq, op0=ALU.mult,
                                                     op1=ALU.subtract), reads=[s2k, msqk], writes=[vark])
        S.op("dve", lambda e: e.tensor_scalar(var, var, 1e-5, None, op0=ALU.add), reads=[vark], writes=[vark])
        sd, sdk = sm()
        act(sd, var, AF.Sqrt, [vark], [sdk])
        rstd, rstdk = sm()
        S.op("dve", lambda e: e.reciprocal(rstd, sd), reads=[sdk], writes=[rstdk])
        S.op("dve", lambda e: e.scalar_tensor_tensor(out=y[:], in0=ysrc[:], scalar=mean, in1=L.gB[:], op0=ALU.subtract,
                                                     op1=ALU.mult), reads=[ysrc.tk, meank, L.gB.tk], writes=[y.tk])
        S.op("dve", lambda e: e.scalar_tensor_tensor(out=y[:], in0=y[:], scalar=rstd, in1=L.bB[:], op0=ALU.mult,
                                                     op1=ALU.add), reads=[y.tk, rstdk, L.bB.tk], writes=[y.tk])
        S.dma(L.store_q, hout_d[t * 128:(t + 1) * 128, :], y[:], reads=[y.tk])
        if hTout_d is None:
            return lambda: None
        hb = L.hb.next()
        if L.cast_eng == "act":
            act(hb[:], y[:], AF.Copy, [y.tk], [hb.tk])
        else:
            S.op("pool", lambda e: e.tensor_copy(hb[:], y[:]), reads=[y.tk], writes=[hb.tk])
        if t % 4 == 0 and L.hTt is not None:
            L.cur_hTt = L.hTt.next()
        hTt = L.cur_hTt
        tq = t % 4

        def fin():
            for g2 in range(2):
                pt = ptrot.next()
                for j in range(8):
                    kt = g2 * 8 + j
                    tp(pt.ap(j * 128, (j + 1) * 128), hb[:, kt * 128:(kt + 1) * 128], [hb.tk], [pt.tk])
                S.op("dve", lambda e: e.tensor_copy(
                    hTt[:, g2 * 8:(g2 + 1) * 8, tq * 128:(tq + 1) * 128],
                    pt.ap(0, 1024).rearrange("p (j q) -> p j q", j=8)), reads=[pt.tk], writes=[hTt.tk])
            if tq == 3:
                t0 = (t - 3) * 128
                S.dma(L.store_q, hTout_d[:, t0:t0 + 512].rearrange("(kt p) q -> p kt q", p=128), hTt[:], reads=[hTt.tk])
        return fin

    def phase_ln_in():
        with ExitStack() as es:
            L = ln_setup(es, W["ln_in_g"], W["ln_in_b"])
            nxt = ln_prefetch(L, x_d, 0)
            pend = None
            for t in range(NT):
                cur = nxt
                if t + 1 < NT:
                    nxt = ln_prefetch(L, x_d, t + 1)
                f = ln_tile(L, t, cur, None, hres[0], hTd[0])
                if pend is not None:
                    pend()
                pend = f
            pend()
            S.barrier()

    def load_rope(es, which):
        R = 64 if which == 64 else 32
        tab = sb(es, f"rope{R}", [R, 2, S_], F32)
        pm = sb(es, f"pm{R}", [R, R], BF16)
        S.dma("sp", tab[:], C[f"rope{R}"], writes=[tab.tk])
        S.dma("sp", pm[:], C[f"p{R}"], writes=[pm.tk])
        return (R, tab, pm)

    def mk_rope_scratch(es):
        return (Rot([sb(es, f"rt1_{i}", [64, 512], F32) for i in range(2)]),
                Rot([sb(es, f"rt2_{i}", [64, 512], F32) for i in range(2)]))

    def apply_rope(dst, dst_tk, rp, pb, c, rsc, roprot):
        R, tab, pm = rp
        pr = roprot.next()
        mm(pr[0:R, :], pm[:, :], dst, True, True, [pm.tk, dst_tk], [pr.tk])
        t1 = rsc[0].next()
        t2 = rsc[1].next()
        S.op("dve", lambda e: e.tensor_tensor(t1[0:R, :], pb[0:R, :], tab[:, 0, c * 512:(c + 1) * 512], ALU.mult),
             reads=[pb.tk, tab.tk], writes=[t1.tk])
        S.op("dve", lambda e: e.tensor_tensor(t2[0:R, :], pr[0:R, :], tab[:, 1, c * 512:(c + 1) * 512], ALU.mult),
             reads=[pr.tk, tab.tk], writes=[t2.tk])
        S.op("pool", lambda e: e.tensor_tensor(dst, t1[0:R, :], t2[0:R, :], ALU.add),
             reads=[t1.tk, t2.tk], writes=[dst_tk])

    deferred_outs = []

    def flush_outs():
        while deferred_outs:
            deferred_outs.pop(0)()

    def attention(c, kts, qk, vaug, v_reads, mode, sbanks, pbufs, obanks, scale, on_final, K=128, hook=None):
        items = []
        for kt in kts:
            lo = max(0, kt - 4 * c)
            hi = 3 if mode == "causal" else min(3, kt + 4 - 4 * c)
            if lo <= hi:
                items.append((kt, lo, hi))

        def stage_a(it):
            kt, lo, hi = it
            c0, c1 = lo * 128, (hi + 1) * 128
            sp = sbanks.next()
            mms = qk(kt, c * 512 + c0, c * 512 + c1)
            masks = []
            for qs in range(lo, hi + 1):
                rel = 4 * c + qs - kt
                m = tri if rel == 0 else (anti if (mode == "win" and rel == 4) else None)
                if m is not None:
                    masks.append((qs, m))
            n = len(mms) + len(masks)
            for i, (l, r, rd) in enumerate(mms):
                mm(sp[0:K, c0:c1], l, r, i == 0, i == n - 1, rd, [sp.tk])
            for j, (qs, m) in enumerate(masks):
                mm(sp[0:K, qs * 128:(qs + 1) * 128], ident[:, 0:K], m[:, :], False, len(mms) + j == n - 1,
                   [ident.tk, m.tk], [sp.tk])
            return sp

        def stage_b(it, sp):
            kt, lo, hi = it
            c0, c1 = lo * 128, (hi + 1) * 128
            pb = pbufs.next()
            act(pb[0:K, c0:c1], sp[0:K, c0:c1], AF.Exp, [sp.tk], [pb.tk], scale=float(scale))
            for qs in range(lo, hi + 1):
                first_kt = 0 if mode == "causal" else max(0, 4 * c + qs - 4)
                last_kt = 4 * c + qs
                ob = obanks[qs]
                mm(ob[:, 0:129], pb[0:K, qs * 128:(qs + 1) * 128], vaug(kt), kt == first_kt, kt == last_kt,
                   [pb.tk] + v_reads, [ob.tk])
                if kt == last_kt:
                    on_final(qs, ob)

        sp_next = stage_a(items[0])
        for i, it in enumerate(items):
            sp = sp_next
            if i + 1 < len(items):
                sp_next = stage_a(items[i + 1])
            stage_b(it, sp)
            if i == 0:
                flush_outs()
            if hook is not None:
                hook()

    def phase_proj(l, hT_src):
        with ExitStack() as es:
            hT = sb(es, "hT", [128, 16, S_], BF16)
            hT_tk = [Tk() for _ in range(4)]
            for g in range(4):
                S.dma("sp", hT[:, :, g * 512:(g + 1) * 512],
                      hT_src[:, g * 512:(g + 1) * 512].rearrange("(kt p) t -> p kt t", p=128), writes=[hT_tk[g]])
            wrot = Rot([sb(es, f"wsl{i}", [128, 16, 512], BF16) for i in range(2)])
            rp32 = load_rope(es, 32)
            rp64 = load_rope(es, 64)
            rsc = mk_rope_scratch(es)
            stage_rot = Rot([(sb(es, f"stg{i}", [128, S_], BF16), [Tk() for _ in range(4)]) for i in range(2)])
            vstage = Rot([sb(es, f"vst{i}", [128, 512], BF16) for i in range(2)])
            cn_rot = Rot([sb(es, f"cn{i}", [128, 512], BF16) for i in range(2)])
            cst_rot = Rot([sb(es, f"cst{i}", [128, 4, 512], BF16) for i in range(2)])
            junk = sb(es, "pjunk", [128, 512], BF16)
            ngs = sb(es, "ngs", [128, 16, 12], F32)
            gB = Rot([sb(es, f"gBq{i}", [128, 512], F32) for i in range(2)])
            psrot = Rot(ps[0:4])
            roprot = Rot(ps[4:6])

            def load_slab(c0, n):
                wb = wrot.next()
                S.dma("pool", wb[:, :, 0:n], W["w_in"][l, :, c0:c0 + n].rearrange("(kt p) n -> p kt n", p=128),
                      writes=[wb.tk])
                return wb

            jc = [0]

            def feat_job(wb, wc0, M, rp, pt_tile):
                jc[0] += 1
                if JOBLIM is not None and jc[0] > JOBLIM:
                    return
                stage, stk = stage_rot.next()
                pend = None
                for c in range(4):
                    pb = psrot.next()
                    for kt in range(16):
                        mm(pb[0:M, :], wb[:, kt, wc0:wc0 + M], hT[:, kt, c * 512:(c + 1) * 512], kt == 0, kt == 15,
                           [wb.tk, hT_tk[c]], [pb.tk])
                    act(stage[0:M, c * 512:(c + 1) * 512], pb[0:M, :], AF.Copy, [pb.tk], [stk[c]])
                    if pend is not None:
                        pend()
                        pend = None
                    if rp is not None:
                        def mk(c=c, pb=pb):
                            R = rp[0]
                            return lambda: apply_rope(stage[0:R, c * 512:(c + 1) * 512], stk[c], rp, pb, c, rsc, roprot)
                        pend = mk()
                if pend is not None:
                    pend()
                S.dma("sp", PT[pt_tile * 128:pt_tile * 128 + M, :], stage[0:M, :], reads=stk)

            def tok_job(wb, wc0, n, kind, dest, g_ap=None):
                jc[0] += 1
                if JOBLIM is not None and jc[0] > JOBLIM:
                    return
                if kind == "rms":
                    g = gB.next()
                    S.dma("sp", g[:], g_ap.partition_broadcast(128), writes=[g.tk])
                cst = None
                pend_fin = [None]
                for t in range(NT):
                    pb = psrot.next()
                    for kt in range(16):
                        mm(pb[:, 0:n], hT[:, kt, t * 128:(t + 1) * 128], wb[:, kt, wc0:wc0 + n], kt == 0, kt == 15,
                           [hT_tk[t // 4], wb.tk], [pb.tk])
                    if kind == "v":
                        vs = vstage.next()
                        act(vs[:, 0:n], pb[:, 0:n], AF.Copy, [pb.tk], [vs.tk])
                        S.dma("sp", VT[t * 128:(t + 1) * 128, dest:dest + n], vs[:, 0:n], reads=[vs.tk])
                    elif kind == "gate":
                        act(ngs[:, t, :], pb[:, 0:12], AF.Sigmoid, [pb.tk], [ngs.tk])
                    else:
                        ss, ssk = zcol()
                        act(junk[:, :], pb[:, :], AF.Square, [pb.tk, ssk], [junk.tk, ssk], accum_out=ss)
                        S.op("dve", lambda e: e.tensor_scalar(ss, ss, 1.0 / 512, 1e-6, op0=ALU.mult, op1=ALU.add),
                             reads=[ssk], writes=[ssk])
                        sd, sdk = sm()
                        act(sd, ss, AF.Sqrt, [ssk], [sdk])
                        r, rk = sm()
                        S.op("dve", lambda e: e.reciprocal(r, sd), reads=[sdk], writes=[rk])
                        cn = cn_rot.next()
                        S.op("dve", lambda e: e.scalar_tensor_tensor(out=cn[:, :], in0=pb[:, :], scalar=r, in1=g[:, :],
                                                                     op0=ALU.mult, op1=ALU.mult),
                             reads=[pb.tk, rk, g.tk], writes=[cn.tk])
                        tq = t % 4
                        if tq == 0:
                            cst = cst_rot.next()

                        def fin(cn=cn, cst=cst, tq=tq, t=t):
                            pt = ptrot.next()
                            for j in range(4):
                                tp(pt.ap(j * 128, (j + 1) * 128), cn[:, j * 128:(j + 1) * 128], [cn.tk], [pt.tk])
                            S.op("dve", lambda e: e.tensor_copy(cst[:, :, tq * 128:(tq + 1) * 128],
                                                                pt.ap(0, 512).rearrange("p (j q) -> p j q", j=4)),
                                 reads=[pt.tk], writes=[cst.tk])
                            if tq == 3:
                                t0 = (t - 3) * 128
                                S.dma("sp", PT[dest * 128:(dest + 4) * 128, t0:t0 + 512].rearrange("(j p) q -> p j q", p=128),
                                      cst[:], reads=[cst.tk])
                        if pend_fin[0] is not None:
                            pend_fin[0]()
                        pend_fin[0] = fin
                if pend_fin[0] is not None:
                    pend_fin[0]()
                    pend_fin[0] = None

            wb = load_slab(0, 512)
            wb2 = load_slab(512, 512)
            tok_job(wb, 0, 512, "rms", 0, W["mla_q_norm"][l])
            wb = load_slab(1024, 64)
            tok_job(wb2, 0, 512, "rms", 4, W["mla_kv_norm"][l])
            wb2 = load_slab(1088, 512)
            feat_job(wb, 0, 64, rp64, 8)
            wb = load_slab(1600, 512)
            for h in range(4):
                feat_job(wb2, h * 128, 128, rp32, 9 + h)
            wb2 = load_slab(2112, 268)
            feat_job(wb, 0, 128, rp32, 13)
            feat_job(wb, 128, 128, None, 14)
            feat_job(wb, 256, 128, rp32, 15)
            tok_job(wb, 384, 128, "v", 0)
            wb = load_slab(2380, 512)
            feat_job(wb2, 0, 128, rp32, 16)
            tok_job(wb2, 128, 128, "v", 128)
            tok_job(wb2, 256, 12, "gate", 0)
            S.dma("sp", NG.rearrange("(t p) g -> p t g", p=128), ngs[:], reads=[ngs.tk])
            wb2 = load_slab(2892, 512)
            for h in range(4):
                feat_job(wb, h * 128, 128, rp32, 17 + h)
            wb = load_slab(3404, 512)
            for h in range(4):
                feat_job(wb2, h * 128, 128, rp32, 21 + h)
            tok_job(wb, 0, 512, "v", 256)
            S.barrier()

    def emit_head_out(onb, c, row0, oTrot, get=None, pr=None):
        pt = (pr or ptrot).next()
        for qs in range(4):
            tp(pt.ap(qs * 128, (qs + 1) * 128), onb[:, qs, :] if get is None else get(qs), [onb.tk], [pt.tk])
        oT = oTrot.next()
        S.op("dve", lambda e: e.tensor_copy(oT[:, :], pt.ap(0, 512)), reads=[pt.tk], writes=[oT.tk])
        S.dma("sp", CATT[row0:row0 + 128, c * 512:(c + 1) * 512], oT[:, :], reads=[oT.tk])

    def phase_mla(l):
        with ExitStack() as es:
            cqT = sb(es, "cqT", [128, 4, S_], BF16)
            ckT = sb(es, "ckT", [128, 4, S_], BF16)
            S.dma("sp", cqT[:], PT[0:512, :].rearrange("(kt p) t -> p kt t", p=128), writes=[cqT.tk])
            S.dma("sp", ckT[:], PT[512:1024, :].rearrange("(kt p) t -> p kt t", p=128), writes=[ckT.tk])
            kr = sb(es, "kr", [64, S_], BF16)
            S.dma("sp", kr[:], PT[1024:1088, :], writes=[kr.tk])
            wuq = sb(es, "wuq", [128, 4, 1536], BF16)
            wukv = sb(es, "wukv", [128, 4, 2048], BF16)
            S.dma("pool", wuq[:], W["mla_w_uq"][l].rearrange("(kt p) n -> p kt n", p=128), writes=[wuq.tk])
            S.dma("pool", wukv[:], W["mla_w_ukv"][l].rearrange("(kt p) n -> p kt n", p=128), writes=[wukv.tk])
            rp64 = load_rope(es, 64)
            rsc = mk_rope_scratch(es)
            heads = Rot([dict(qn=sb(es, f"qn{i}", [128, S_], BF16), qr=sb(es, f"qr{i}", [64, S_], BF16),
                              kn=sb(es, f"kn{i}", [128, S_], BF16), vh=sb(es, f"vh{i}", [128, 16, 129], BF16),
                              qrk=[Tk() for _ in range(4)]) for i in range(2)])
            for hb in heads.bufs:
                S.op("pool", lambda e: e.memset(hb["vh"][:, :, 128:129], 1.0), writes=[hb["vh"].tk])
            pbufs = Rot([sb(es, f"pT{i}", [128, 512], BF16) for i in range(3)])
            onbs = Rot([sb(es, f"onb{i}", [128, 4, 128], BF16) for i in range(2)])
            oTrot = Rot([sb(es, f"oT{i}", [128, 512], BF16) for i in range(2)])
            sbanks = Rot(ps[0:2])
            obanks = ps[2:6]
            projrot = Rot(ps[0:6])
            scale = 192.0 ** -0.5
            precast_plan(l)
            for h in range(8):
                precast(2)
                H = heads.next()
                qn, qr, kn, vh, qrk = H["qn"], H["qr"], H["kn"], H["vh"], H["qrk"]
                for c in range(4):
                    cs = slice(c * 512, (c + 1) * 512)
                    pbr = projrot.next()
                    for kt in range(4):
                        mm(pbr[0:64, :], wuq[:, kt, 192 * h + 128:192 * h + 192], cqT[:, kt, cs], kt == 0, kt == 3,
                           [wuq.tk, cqT.tk], [pbr.tk])
                    act(qr[:, cs], pbr[0:64, :], AF.Copy, [pbr.tk], [qrk[c]])
                    pb = projrot.next()
                    for kt in range(4):
                        mm(pb[:, :], wuq[:, kt, 192 * h:192 * h + 128], cqT[:, kt, cs], kt == 0, kt == 3,
                           [wuq.tk, cqT.tk], [pb.tk])
                    act(qn[:, cs], pb[:, :], AF.Copy, [pb.tk], [qn.tk])
                    pb = projrot.next()
                    for kt in range(4):
                        mm(pb[:, :], wukv[:, kt, 256 * h:256 * h + 128], ckT[:, kt, cs], kt == 0, kt == 3,
                           [wukv.tk, ckT.tk], [pb.tk])
                    act(kn[:, cs], pb[:, :], AF.Copy, [pb.tk], [kn.tk])
                    apply_rope(qr[:, cs], qrk[c], rp64, pbr, c, rsc, ptrot_raw)
                for t in range(NT):
                    pb = projrot.next()
                    for kt in range(4):
                        mm(pb[:, 0:128], ckT[:, kt, t * 128:(t + 1) * 128], wukv[:, kt, 256 * h + 128:256 * h + 256],
                           kt == 0, kt == 3, [wukv.tk, ckT.tk], [pb.tk])
                    act(vh[:, t, 0:128], pb[:, 0:128], AF.Copy, [pb.tk], [vh.tk])

                def qk(kt, q0, q1):
                    return [(kn[:, kt * 128:(kt + 1) * 128], qn[:, q0:q1], [kn.tk, qn.tk]),
                            (kr[:, kt * 128:(kt + 1) * 128], qr[:, q0:q1], [kr.tk] + qrk)]

                for c in range(4):
                    onb = onbs.next()

                    def on_final(qs, ob):
                        rv, rvk = sm()
                        S.op("dve", lambda e: e.reciprocal(rv, ob[:, 128:129]), reads=[ob.tk], writes=[rvk])
                        S.op("dve", lambda e: e.tensor_scalar(onb[:, qs, :], ob[:, 0:128], rv, None, op0=ALU.mult),
                             reads=[ob.tk, rvk], writes=[onb.tk])
                        if qs == 3:
                            deferred_outs.append(lambda onb=onb, c=c, h=h: emit_head_out(onb, c, h * 128, oTrot))

                    attention(c, range(0, 4 * c + 4), qk, lambda kt: vh[:, kt, :], [vh.tk], "causal", sbanks, pbufs,
                              obanks, scale, on_final)
            flush_outs()
            S.barrier()

    def phase_nsa(l):
        with ExitStack() as es:
            nqT = sb(es, "nqT", [128, 4, S_], BF16)
            S.dma("sp", nqT[:], PT[9 * 128:13 * 128, :].rearrange("(h p) t -> p h t", p=128), writes=[nqT.tk])
            srcs = []
            for i, tl in enumerate((13, 14, 15, 16)):
                b = sb(es, f"nsrc{i}", [128, S_], BF16)
                S.dma("sp", b[:], PT[tl * 128:(tl + 1) * 128, :], writes=[b.tk])
                srcs.append(b)
            kcsrc, vcsrc, ksT, kwT = srcs
            vs = sb(es, "vs", [128, 16, 129], BF16)
            vw = sb(es, "vw", [128, 16, 129], BF16)
            S.dma("sp", vs[:, :, 0:128], VT[:, 0:128].rearrange("(t p) d -> p t d", p=128), writes=[vs.tk])
            S.dma("sp", vw[:, :, 0:128], VT[:, 128:256].rearrange("(t p) d -> p t d", p=128), writes=[vw.tk])
            S.op("pool", lambda e: e.memset(vs[:, :, 128:129], 1.0), writes=[vs.tk])
            S.op("pool", lambda e: e.memset(vw[:, :, 128:129], 1.0), writes=[vw.tk])
            w1b = [sb(es, f"cw1_{i}", [128, 32, 128], BF16) for i in range(2)]
            w2b = [sb(es, f"cw2_{i}", [128, 128], BF16) for i in range(2)]
            posb = [sb(es, f"cpos_{i}", [32, 128], BF16) for i in range(2)]
            posT = [sb(es, f"cposT_{i}", [128, 32], BF16) for i in range(2)]
            for i in range(2):
                S.dma("pool", w1b[i][:], W["nsa_cmp_w1"][l, i].rearrange("(l d) j -> d l j", d=128), writes=[w1b[i].tk])
                S.dma("pool", w2b[i][:], W["nsa_cmp_w2"][l, i], writes=[w2b[i].tk])
                S.dma("pool", posb[i][:], W["nsa_cmp_pos"][l, i], writes=[posb[i].tk])
            cmask = sb(es, "cmask", [128, S_], BF16)
            S.dma("sp", cmask[:], C["cmpmask"], writes=[cmask.tk])
            ntab = sb(es, "ntab", [128, 16, 2, 32], F32)
            S.dma("sp", ntab[:], C["nsatab"], writes=[ntab.tk])
            ex = sb(es, "ex", [32, S_], BF16)
            S.dma("sp", ex[:], C["ex"], writes=[ex.tk])
            ng = sb(es, "ng", [128, 16, 12], F32)
            S.dma("sp", ng[:], NG.rearrange("(t p) g -> p t g", p=128), writes=[ng.tk])
            kcT = sb(es, "kcT", [128, 128], BF16)
            vcaug = sb(es, "vcaug", [128, 161], BF16)
            S.dma("sp", vcaug[:, 128:161], C["ov"], writes=[vcaug.tk])
            impS = sb(es, "impS", [128, 16, 32], F32)
            acc_rot = Rot([sb(es, f"acc{i}", [128, 4, 4, 128], F32) for i in range(2)])
            accb_rot = Rot([sb(es, f"accb{i}", [128, 4, 4, 128], BF16) for i in range(2)])
            biasT_rot = Rot([sb(es, f"biasT{i}", [32, 512], BF16) for i in range(2)])
            pbufs = Rot([sb(es, f"pT{i}", [128, 512], BF16) for i in range(3)])
            oTrot = Rot([sb(es, f"oT{i}", [128, 512], BF16) for i in range(2)])
            xg = sb(es, "xg", [128, 128], F32)
            ug = sb(es, "ug", [128, 128], F32)
            hidb = [sb(es, f"hidb{i}", [128, 128], BF16) for i in range(2)]
            scr = Rot([sb(es, f"scr{i}", [128, 32], F32) for i in range(2)])
            scr2 = Rot([sb(es, f"scr2_{i}", [128, 32], F32) for i in range(2)])
            mx = Rot([sb(es, f"mx{i}", [128, 16], F32) for i in range(2)])
            bsb = [sb(es, f"bsb{i}", [128, 32], BF16) for i in range(4)]
            sbanks = Rot(ps[0:2])
            obanks = ps[2:6]
            scale = 128.0 ** -0.5

            for i in range(2):
                pt = ptrot.next()
                tp(pt.pap(128, 0, 32), posb[i][:, :], [posb[i].tk], [pt.tk], kp=32)
                S.op("dve", lambda e: e.tensor_copy(posT[i][:, :], pt.ap(0, 32)), reads=[pt.tk], writes=[posT[i].tk])
                src = kcsrc if i == 0 else vcsrc
                v3 = src[:, :].rearrange("p (n s) -> p n s", s=16)
                pa = sbanks.next()
                for ll in range(32):
                    rhs = v3[:, 0:127, ll] if ll < 16 else v3[:, 1:128, ll - 16]
                    mm(pa[:, 0:127], w1b[i][:, ll, :], rhs, ll == 0, ll == 31, [w1b[i].tk, src.tk], [pa.tk])
                pbk = sbanks.next()
                for ll in range(32):
                    mm(pbk[:, 0:1], w1b[i][:, ll, :], posT[i][:, ll:ll + 1], ll == 0, ll == 31,
                       [w1b[i].tk, posT[i].tk], [pbk.tk])
                cst, cstk = sm()
                S.op("dve", lambda e: e.tensor_copy(cst, pbk[:, 0:1]), reads=[pbk.tk], writes=[cstk])
                S.op("dve", lambda e: e.tensor_scalar(xg[:, 0:127], pa[:, 0:127], cst, None, op0=ALU.add),
                     reads=[pa.tk, cstk], writes=[xg.tk])
                S.op("dve", lambda e: e.tensor_tensor(ug[:, 0:127], xg[:, 0:127], xg[:, 0:127], ALU.mult),
                     reads=[xg.tk], writes=[ug.tk])
                S.op("dve", lambda e: e.tensor_scalar(ug[:, 0:127], ug[:, 0:127], 0.0713548162726, 1.5957691216057308,
                                                      op0=ALU.mult, op1=ALU.add), reads=[ug.tk], writes=[ug.tk])
                S.op("dve", lambda e: e.tensor_tensor(ug[:, 0:127], ug[:, 0:127], xg[:, 0:127], ALU.mult),
                     reads=[ug.tk, xg.tk], writes=[ug.tk])
                act(ug[:, 0:127], ug[:, 0:127], AF.Sigmoid, [ug.tk], [ug.tk])
                S.op("dve", lambda e: e.tensor_tensor(hidb[i][:, 0:127], ug[:, 0:127], xg[:, 0:127], ALU.mult),
                     reads=[ug.tk, xg.tk], writes=[hidb[i].tk])
                po = sbanks.next()
                if i == 0:
                    mm(po[:, 0:127], w2b[0][:, :], hidb[0][:, 0:127], True, True, [w2b[0].tk, hidb[0].tk], [po.tk])
                    act(kcT[:, 0:127], po[:, 0:127], AF.Copy, [po.tk], [kcT.tk])
                else:
                    mm(po[0:127, 0:128], hidb[1][:, 0:127], w2b[1][:, :], True, True, [w2b[1].tk, hidb[1].tk], [po.tk])
                    act(vcaug[0:127, 0:128], po[0:127, 0:128], AF.Copy, [po.tk], [vcaug.tk])

            for c in range(4):
                cs = slice(c * 512, (c + 1) * 512)
                acc = acc_rot.next()
                cmpb = [ps[6], ps[7]]

                def cmp_head(h):
                    sp = sbanks.next()
                    mm(sp[0:127, :], kcT[:, 0:127], nqT[:, h, cs], True, False, [kcT.tk, nqT.tk], [sp.tk])
                    mm(sp[0:127, :], ident[:, 0:127], cmask[:, cs], False, True, [ident.tk, cmask.tk], [sp.tk])
                    pb = pbufs.next()
                    act(pb[0:127, :], sp[0:127, :], AF.Exp, [sp.tk], [pb.tk], scale=float(scale))
                    for qs in range(4):
                        T = 4 * c + qs
                        ob = cmpb[qs % 2]
                        mm(ob[:, 0:161], pb[0:127, qs * 128:(qs + 1) * 128], vcaug[0:127, 0:161], True, True,
                           [pb.tk, vcaug.tk], [ob.tk])
                        rv, rvk = sm()
                        S.op("dve", lambda e: e.tensor_scalar(rv, ob[:, 128:129], 1e-30, None, op0=ALU.max),
                             reads=[ob.tk], writes=[rvk])
                        S.op("dve", lambda e: e.reciprocal(rv, rv), reads=[rvk], writes=[rvk])
                        cf, cfk = sm()
                        S.op("dve", lambda e: e.tensor_tensor(cf, rv, ng[:, T, 3 * h:3 * h + 1], ALU.mult),
                             reads=[rvk, ng.tk], writes=[cfk])
                        S.op("dve", lambda e: e.tensor_scalar(acc[:, qs, h, :], ob[:, 0:128], cf, None, op0=ALU.mult),
                             reads=[ob.tk, cfk], writes=[acc.tk])
                        if h == 0:
                            S.op("dve", lambda e: e.tensor_scalar(impS[:, T, :], ob[:, 129:161], rv, None, op0=ALU.mult),
                                 reads=[ob.tk, rvk], writes=[impS.tk])
                        else:
                            S.op("dve", lambda e: e.scalar_tensor_tensor(out=impS[:, T, :], in0=ob[:, 129:161], scalar=rv,
                                                                         in1=impS[:, T, :], op0=ALU.mult, op1=ALU.add),
                                 reads=[ob.tk, rvk, impS.tk], writes=[impS.tk])

                def mk_final(h, gi):
                    def on_final(qs, ob):
                        T = 4 * c + qs
                        rv, rvk = sm()
                        S.op("dve", lambda e: e.reciprocal(rv, ob[:, 128:129]), reads=[ob.tk], writes=[rvk])
                        cf, cfk = sm()
                        S.op("dve", lambda e: e.tensor_tensor(cf, rv, ng[:, T, 3 * h + gi:3 * h + gi + 1], ALU.mult),
                             reads=[rvk, ng.tk], writes=[cfk])
                        S.op("dve", lambda e: e.scalar_tensor_tensor(out=acc[:, qs, h, :], in0=ob[:, 0:128], scalar=cf,
                                                                     in1=acc[:, qs, h, :], op0=ALU.mult, op1=ALU.add),
                             reads=[ob.tk, cfk, acc.tk], writes=[acc.tk])
                    return on_final

                def win_head(h):
                    def qk_win(kt, q0, q1):
                        return [(kwT[:, kt * 128:(kt + 1) * 128], nqT[:, h, q0:q1], [kwT.tk, nqT.tk])]
                    attention(c, range(max(0, 4 * c - 4), 4 * c + 4), qk_win, lambda kt: vw[:, kt, :], [vw.tk], "win",
                              sbanks, pbufs, obanks, scale, mk_final(h, 2))

                for h in range(4):
                    cmp_head(h)
                    if h > 0:
                        win_head(h - 1)
                biasT = None
                bbs = []
                if c >= 2:
                    biasT = biasT_rot.next()
                    for qs in range(4):
                        T = 4 * c + qs
                        sc = scr.next()
                        S.op("dve", lambda e: e.tensor_tensor(sc[:, :], impS[:, T, :], ntab[:, T, 0, :], ALU.mult),
                             reads=[impS.tk, ntab.tk], writes=[sc.tk])
                        S.op("dve", lambda e: e.tensor_tensor(sc[:, :], sc[:, :], ntab[:, T, 1, :], ALU.add),
                             reads=[sc.tk, ntab.tk], writes=[sc.tk])
                        m8 = mx.next()
                        S.op("dve", lambda e: e.max(out=m8[:, 0:8], in_=sc[:, :]), reads=[sc.tk], writes=[m8.tk])
                        sc2 = scr2.next()
                        S.op("dve", lambda e: e.match_replace(out=sc2[:, :], in_to_replace=m8[:, 0:8], in_values=sc[:, :],
                                                              imm_value=-1e30), reads=[sc.tk, m8.tk], writes=[sc2.tk])
                        S.op("dve", lambda e: e.max(out=m8[:, 8:16], in_=sc2[:, :]), reads=[sc2.tk], writes=[m8.tk])
                        S.op("dve", lambda e: e.tensor_scalar(sc2[:, :], sc[:, :], m8[:, 15:16], None, op0=ALU.is_ge),
                             reads=[sc.tk, m8.tk], writes=[sc2.tk])
                        bb = bsb[qs]
                        S.op("dve", lambda e: e.tensor_scalar(bb[:, :], sc2[:, :], -1.0, -NEG, op0=ALU.add, op1=ALU.mult),
                             reads=[sc2.tk], writes=[bb.tk])
                        bbs.append(bb)
                win_head(3)
                if c >= 2:
                    pt = ptrot.next()
                    for qs in range(4):
                        tp(pt.pap(32, qs * 128, (qs + 1) * 128), bbs[qs][:, :], [bbs[qs].tk], [pt.tk])
                    S.op("dve", lambda e: e.tensor_copy(biasT[:, :], pt.pap(32, 0, 512)), reads=[pt.tk], writes=[biasT.tk])
                for h in range(4):
                    precast(1)

                    def qk_sel(kt, q0, q1):
                        r = [(ksT[:, kt * 128:(kt + 1) * 128], nqT[:, h, q0:q1], [ksT.tk, nqT.tk])]
                        if biasT is not None:
                            r.append((ex[:, kt * 128:(kt + 1) * 128], biasT[:, q0 - c * 512:q1 - c * 512],
                                      [ex.tk, biasT.tk]))
                        return r

                    attention(c, range(0, 4 * c + 4), qk_sel, lambda kt: vs[:, kt, :], [vs.tk], "causal", sbanks, pbufs,
                              obanks, scale, mk_final(h, 1))
                accb = accb_rot.next()
                act(accb[:, :, :, :], acc[:, :, :, :], AF.Copy, [acc.tk], [accb.tk])
                for h in range(4):
                    emit_head_out(accb, c, (8 + h) * 128, oTrot, get=lambda qs: accb[:, qs, h, :])
            S.barrier()

    def phase_moba(l):
        with ExitStack() as es:
            mqT = sb(es, "mqT", [128, 4, S_], BF16)
            mkT = sb(es, "mkT", [128, 4, S_], BF16)
            S.dma("sp", mqT[:], PT[17 * 128:21 * 128, :].rearrange("(h p) t -> p h t", p=128), writes=[mqT.tk])
            S.dma("sp", mkT[:], PT[21 * 128:25 * 128, :].rearrange("(h p) t -> p h t", p=128), writes=[mkT.tk])
            mva = sb(es, "mva", [128, 16, 4, 129], BF16)
            for h in range(4):
                S.dma("sp", mva[:, :, h, 0:128], VT[:, 256 + h * 128:256 + (h + 1) * 128].rearrange("(t p) d -> p t d", p=128),
                      writes=[mva.tk])
            S.op("pool", lambda e: e.memset(mva[:, :, :, 128:129], 1.0), writes=[mva.tk])
            mtab = sb(es, "mtab", [128, 16, 3, 8], F32)
            S.dma("sp", mtab[:], C["mobtab"], writes=[mtab.tk])
            ex2 = sb(es, "ex2", [8, 1024], BF16)
            S.dma("sp", ex2[:], C["ex2"], writes=[ex2.tk])
            km = sb(es, "km", [128, 8], F32)
            kmh = sb(es, "kmh", [128, 8], BF16)
            kml = sb(es, "kml", [128, 8], BF16)
            kmd = sb(es, "kmd", [128, 8], F32)
            gm_rot = Rot([sb(es, f"gm{i}", [128, 8], F32) for i in range(2)])
            s01_rot = Rot([sb(es, f"s01{i}", [128, 8], F32) for i in range(2)])
            m8_rot = Rot([sb(es, f"mm8{i}", [128, 8], F32) for i in range(2)])
            bb_rot = Rot([sb(es, f"mbb{i}", [128, 8], BF16) for i in range(2)])
            biasT_rot = Rot([sb(es, f"mbiasT{i}", [8, S_], BF16) for i in range(2)])
            pbufs = Rot([sb(es, f"pT{i}", [128, 512], BF16) for i in range(3)])
            onbs = Rot([sb(es, f"onb{i}", [128, 4, 128], BF16) for i in range(2)])
            oTrot = Rot([sb(es, f"oT{i}", [128, 512], BF16) for i in range(2)])
            sbanks = Rot(ps[0:2])
            obanks = ps[2:6]
            scale = 128.0 ** -0.5
            pgb = ps[6]
            ptl = Rot([PTBank(ps[7])])
            kms = Rot([dict(km=sb(es, f"km{i}", [128, 8], F32), kmh=sb(es, f"kmh{i}", [128, 8], BF16),
                            kml=sb(es, f"kml{i}", [128, 8], BF16), kmd=sb(es, f"kmd{i}", [128, 8], F32)) for i in range(2)])

            def gate_items(h):
                KM = kms.next()
                km_, kmh_, kml_, kmd_ = KM["km"], KM["kmh"], KM["kml"], KM["kmd"]
                biasT = biasT_rot.next()
                items = []

                def prep():
                    S.op("dve", lambda e: e.tensor_reduce(out=km_[:, :], in_=mkT[:, h, :].rearrange("p (n k) -> p n k", k=256),
                                                          axis=AX.X, op=ALU.add), reads=[mkT.tk], writes=[km_.tk])
                    S.op("dve", lambda e: e.tensor_scalar(km_[:, :], km_[:, :], 1.0 / 256, None, op0=ALU.mult),
                         reads=[km_.tk], writes=[km_.tk])
                    S.op("dve", lambda e: e.tensor_copy(kmh_[:, :], km_[:, :]), reads=[km_.tk], writes=[kmh_.tk])
                    S.op("dve", lambda e: e.tensor_tensor(kmd_[:, :], km_[:, :], kmh_[:, :], ALU.subtract),
                         reads=[km_.tk, kmh_.tk], writes=[kmd_.tk])
                    S.op("dve", lambda e: e.tensor_copy(kml_[:, :], kmd_[:, :]), reads=[kmd_.tk], writes=[kml_.tk])
                items.append(prep)
                state = {}
                for T in range(16):
                    def part1(T=T):
                        pg = pgb
                        mm(pg[:, 0:8], mqT[:, h, T * 128:(T + 1) * 128], kmh_[:, :], True, False, [mqT.tk, kmh_.tk], [pg.tk])
                        mm(pg[:, 0:8], mqT[:, h, T * 128:(T + 1) * 128], kml_[:, :], False, True, [mqT.tk, kml_.tk], [pg.tk])
                        gm = gm_rot.next()
                        S.op("dve", lambda e: e.tensor_tensor(gm[:, :], pg[:, 0:8], mtab[:, T, 0, :], ALU.mult),
                             reads=[pg.tk, mtab.tk], writes=[gm.tk])
                        S.op("dve", lambda e: e.tensor_tensor(gm[:, :], gm[:, :], mtab[:, T, 1, :], ALU.add),
                             reads=[gm.tk, mtab.tk], writes=[gm.tk])
                        m8 = m8_rot.next()
                        S.op("dve", lambda e: e.max(out=m8[:, :], in_=gm[:, :]), reads=[gm.tk], writes=[m8.tk])
                        s01 = s01_rot.next()
                        S.op("dve", lambda e: e.tensor_scalar(s01[:, :], gm[:, :], m8[:, 2:3], None, op0=ALU.is_ge),
                             reads=[gm.tk, m8.tk], writes=[s01.tk])
                        S.op("dve", lambda e: e.tensor_tensor(s01[:, :], s01[:, :], mtab[:, T, 0, :], ALU.mult),
                             reads=[s01.tk, mtab.tk], writes=[s01.tk])
                        S.op("dve", lambda e: e.tensor_tensor(s01[:, :], s01[:, :], mtab[:, T, 2, :], ALU.add),
                             reads=[s01.tk, mtab.tk], writes=[s01.tk])
                        bb = bb_rot.next()
                        S.op("dve", lambda e: e.tensor_scalar(bb[:, :], s01[:, :], -1.0, -NEG, op0=ALU.add, op1=ALU.mult),
                             reads=[s01.tk], writes=[bb.tk])
                        state[T] = bb

                    def part2(T=T):
                        pt = ptl.next()
                        bb = state[T]
                        tp(pt.pap(8, 0, 128), bb[:, :], [bb.tk], [pt.tk])
                        S.op("dve", lambda e: e.tensor_copy(biasT[:, T * 128:(T + 1) * 128], pt.pap(8, 0, 128)),
                             reads=[pt.tk], writes=[biasT.tk])
                    items.append(part1)
                    items.append(part2)
                p1s, p2s = items[1::2], items[2::2]
                order_ = [items[0], p1s[0]]
                for T in range(1, 16):
                    order_ += [p1s[T], p2s[T - 1]]
                order_.append(p2s[15])
                return biasT, order_

            biasT, its = gate_items(0)
            for f in its:
                f()
            for h in range(4):
                if h + 1 < 4:
                    biasT_next, pending = gate_items(h + 1)
                else:
                    biasT_next, pending = None, []
                pending = list(pending)

                def hook():
                    if pending:
                        pending.pop(0)()

                def qk(kt, q0, q1):
                    nblk = kt // 2
                    return [(mkT[:, h, kt * 128:(kt + 1) * 128], mqT[:, h, q0:q1], [mkT.tk, mqT.tk]),
                            (ex2[:, nblk * 128:(nblk + 1) * 128], biasT[:, q0:q1], [ex2.tk, biasT.tk])]

                for c in range(4):
                    precast(1)
                    onb = onbs.next()

                    def on_final(qs, ob):
                        rv, rvk = sm()
                        S.op("dve", lambda e: e.reciprocal(rv, ob[:, 128:129]), reads=[ob.tk], writes=[rvk])
                        S.op("dve", lambda e: e.tensor_scalar(onb[:, qs, :], ob[:, 0:128], rv, None, op0=ALU.mult),
                             reads=[ob.tk, rvk], writes=[onb.tk])
                        if qs == 3:
                            deferred_outs.append(lambda onb=onb, c=c, h=h: emit_head_out(onb, c, (12 + h) * 128, oTrot, pr=ptl))

                    attention(c, range(0, 4 * c + 4), qk, lambda kt: mva[:, kt, h, :], [mva.tk], "causal", sbanks, pbufs,
                              obanks, scale, on_final, hook=hook)
                while pending:
                    pending.pop(0)()
                biasT = biasT_next
            flush_outs()
            precast(1000)
            S.barrier()

    def phase_proj_ln(XT_d, W_ap, hin_d, g_ap, b_ap, hout_d, hTout_d):
        with ExitStack() as es:
            Wb = sb(es, "Wb", [128, 16, D], BF16)
            Wtk = [Tk() for _ in range(4)]
            for c4 in range(4):
                S.dma("pool", Wb[:, :, c4 * 512:(c4 + 1) * 512],
                      W_ap[:, c4 * 512:(c4 + 1) * 512].rearrange("(kt p) n -> p kt n", p=128), writes=[Wtk[c4]])
            L = ln_setup(es, g_ap, b_ap, slim=True)
            xrot = Rot([sb(es, f"xTt{i}", [128, 16, 512], BF16) for i in range(2)])

            def load_x(tb):
                xb = xrot.next()
                S.dma("sp", xb[:], XT_d[:, tb * 512:(tb + 1) * 512].rearrange("(kt p) q -> p kt q", p=128), writes=[xb.tk])
                return xb

            nx = load_x(0)
            nh = ln_prefetch(L, hin_d, 0)
            pend = None
            xb = None
            for t in range(NT):
                if t % 4 == 0:
                    xb = nx
                    if t + 4 < NT:
                        nx = load_x(t // 4 + 1)
                hb_ = nh
                if t + 1 < NT:
                    nh = ln_prefetch(L, hin_d, t + 1)
                tq = t % 4
                for c4 in range(4):
                    for kt in range(16):
                        mm(ps[c4][:, :], xb[:, kt, tq * 128:(tq + 1) * 128], Wb[:, kt, c4 * 512:(c4 + 1) * 512], kt == 0, kt == 15,
                           [xb.tk, Wtk[c4]], [ps[c4].tk])
                prev = pend
                pend = ln_tile(L, t, hb_, ps[0:4], hout_d, hTout_d)
                if prev is not None:
                    prev()
            pend()
            S.barrier()

    def phase_memT():
        with ExitStack() as es:
            mf = sb(es, "memf", [128, 2, D], F32)
            S.dma("sp", mf[:], mem_d.rearrange("(mt p) d -> p mt d", p=128), writes=[mf.tk])
            mb = sb(es, "memb", [128, 2, D], BF16)
            act(mb[:], mf[:], AF.Copy, [mf.tk], [mb.tk])
            mT = sb(es, "memT", [128, 16, 256], BF16)
            for mt in range(2):
                for g2 in range(2):
                    pt = ptrot.next()
                    for j in range(8):
                        kt = g2 * 8 + j
                        tp(pt.ap(j * 128, (j + 1) * 128), mb[:, mt, kt * 128:(kt + 1) * 128], [mb.tk], [pt.tk])
                    S.op("dve", lambda e: e.tensor_copy(mT[:, g2 * 8:(g2 + 1) * 8, mt * 128:(mt + 1) * 128],
                                                        pt.ap(0, 1024).rearrange("p (j q) -> p j q", j=8)),
                         reads=[pt.tk], writes=[mT.tk])
            S.dma("sp", MEMT.rearrange("(kt p) m -> p kt m", p=128), mT[:], reads=[mT.tk])
            S.barrier()

    def phase_cross(l, hT_src):
        with ExitStack() as es:
            hT = sb(es, "hT", [128, 16, S_], BF16)
            hT_tk = [Tk() for _ in range(4)]
            for g in range(4):
                S.dma("sp", hT[:, :, g * 512:(g + 1) * 512],
                      hT_src[:, g * 512:(g + 1) * 512].rearrange("(kt p) t -> p kt t", p=128), writes=[hT_tk[g]])
            mT = sb(es, "memT", [128, 16, 256], BF16)
            S.dma("sp", mT[:], MEMT.rearrange("(kt p) m -> p kt m", p=128), writes=[mT.tk])
            wrot = Rot([sb(es, f"wsl{i}", [128, 16, 512], BF16) for i in range(2)])
            kxT = sb(es, "kxT", [128, 16, 256], BF16)
            vx = sb(es, "vx", [128, 2, D], BF16)
            ones = sb(es, "ones", [128, 2], BF16)
            S.op("pool", lambda e: e.memset(ones[:, :], 1.0), writes=[ones.tk])
            qx_rot = Rot([sb(es, f"qx{i}", [128, 4, 512], BF16) for i in range(2)])
            pT_rot = Rot([sb(es, f"pTx{i}", [128, 2, 512], BF16) for i in range(2)])
            ox_rot = Rot([sb(es, f"ox{i}", [128, 4, 512], BF16) for i in range(2)])
            oTrot = Rot([sb(es, f"oT{i}", [128, 512], BF16) for i in range(2)])
            sbanks = Rot(ps[0:2])
            obanks = ps[2:6]
            sumb = ps[6]
            ptx = PTBank(ps[7])
            scale = 512.0 ** -0.5

            def load_slab(Wap, c0):
                wb = wrot.next()
                S.dma("pool", wb[:, :, :], Wap[:, c0:c0 + 512].rearrange("(kt p) n -> p kt n", p=128), writes=[wb.tk])
                return wb

            wkv = WKVB
            wq = WQB
            nxt = load_slab(wkv, 0)
            for sl in range(8):
                wb = nxt
                if sl + 1 < 8:
                    nxt = load_slab(wkv, (sl + 1) * 512)
                else:
                    nxt = load_slab(wq, 0)
                if sl < 4:
                    for j in range(4):
                        pb = sbanks.next()
                        for kt in range(16):
                            mm(pb[:, 0:256], wb[:, kt, j * 128:(j + 1) * 128], mT[:, kt, :], kt == 0, kt == 15,
                               [wb.tk, mT.tk], [pb.tk])
                        act(kxT[:, sl * 4 + j, :], pb[:, 0:256], AF.Copy, [pb.tk], [kxT.tk])
                else:
                    for mt in range(2):
                        pb = sbanks.next()
                        for kt in range(16):
                            mm(pb[:, :], mT[:, kt, mt * 128:(mt + 1) * 128], wb[:, kt, :], kt == 0, kt == 15,
                               [wb.tk, mT.tk], [pb.tk])
                        act(vx[:, mt, (sl - 4) * 512:(sl - 3) * 512], pb[:, :], AF.Copy, [pb.tk], [vx.tk])
            def qproj(wb, c):
                cs = slice(c * 512, (c + 1) * 512)
                qx = qx_rot.next()
                for j in range(4):
                    pb = sbanks.next()
                    for kt in range(16):
                        mm(pb[:, :], wb[:, kt, j * 128:(j + 1) * 128], hT[:, kt, cs], kt == 0, kt == 15,
                           [wb.tk, hT_tk[c]], [pb.tk])
                    act(qx[:, j, :], pb[:, :], AF.Copy, [pb.tk], [qx.tk])
                return qx

            def xattn(h, c, qx):
                cs = slice(c * 512, (c + 1) * 512)
                pT = pT_rot.next()
                for mt in range(2):
                    sp = sbanks.next()
                    for j in range(4):
                        mm(sp[:, :], kxT[:, h * 4 + j, mt * 128:(mt + 1) * 128], qx[:, j, :], j == 0, j == 3,
                           [kxT.tk, qx.tk], [sp.tk])
                    act(pT[:, mt, :], sp[:, :], AF.Exp, [sp.tk], [pT.tk], scale=float(scale))
                ox = ox_rot.next()
                for qs in range(4):
                    ob = obanks[qs]
                    for mt in range(2):
                        mm(ob[:, :], pT[:, mt, qs * 128:(qs + 1) * 128], vx[:, mt, h * 512:(h + 1) * 512], mt == 0, mt == 1,
                           [pT.tk, vx.tk], [ob.tk])
                    for mt in range(2):
                        mm(sumb[:, qs:qs + 1], pT[:, mt, qs * 128:(qs + 1) * 128], ones[:, 0:1], mt == 0, mt == 1,
                           [pT.tk, ones.tk], [sumb.tk])
                    rv, rvk = sm()
                    S.op("dve", lambda e: e.reciprocal(rv, sumb[:, qs:qs + 1]), reads=[sumb.tk], writes=[rvk])
                    S.op("dve", lambda e: e.tensor_scalar(ox[:, qs, :], ob[:, :], rv, None, op0=ALU.mult),
                         reads=[ob.tk, rvk], writes=[ox.tk])

                def outs():
                    for j in range(4):
                        for qs in range(4):
                            tp(ptx.ap(qs * 128, (qs + 1) * 128), ox[:, qs, j * 128:(j + 1) * 128], [ox.tk], [ptx.tk])
                        oT = oTrot.next()
                        S.op("dve", lambda e: e.tensor_copy(oT[:, :], ptx.ap(0, 512)), reads=[ptx.tk], writes=[oT.tk])
                        r0 = (h * 4 + j) * 128
                        S.dma("sp", XOT[r0:r0 + 128, cs], oT[:, :], reads=[oT.tk])
                return outs

            steps = [(h, c) for h in range(4) for c in range(4)]
            slabs = {0: nxt}
            qx_cur = qproj(slabs[0], 0)
            pend_out = None
            for i, (h, c) in enumerate(steps):
                qx_next = None
                if c == 0 and h + 1 < 4:
                    slabs[h + 1] = load_slab(wq, (h + 1) * 512)
                if i + 1 < len(steps):
                    h2, c2 = steps[i + 1]
                    qx_next = qproj(slabs[h2], c2)
                if pend_out is not None:
                    pend_out()
                pend_out = xattn(h, c, qx_cur)
                qx_cur = qx_next
            pend_out()
            S.barrier()

    def phase_mlp(l, hT_src, hin_d, g_ap, b_ap, hout_d, hTout_d):
        with ExitStack() as es:
            L = LNCtx()
            L.store_q = "sp"
            L.cast_eng = "act"
            L.gB = sb(es, "ln_gB", [128, D], F32)
            L.bB = sb(es, "ln_bB", [128, D], F32)
            S.dma("sp", L.gB[:], g_ap.partition_broadcast(128), writes=[L.gB.tk])
            S.dma("sp", L.bB[:], b_ap.partition_broadcast(128), writes=[L.bB.tk])
            hbj = sb(es, "ln_hbj", [128, D], BF16)
            L.junk = hbj
            L.hb = Rot([hbj])
            L.hTt = None
            hbufs = Rot([sb(es, f"hbuf{i}", [128, 16, 512], BF16) for i in range(2)])
            HT = sb(es, "HT", [128, 64, 512], BF16)
            HTk = [Tk() for _ in range(64)]
            w1rot = Rot([sb(es, f"w1s{i}", [128, 16, 256], BF16) for i in range(2)])
            w2rot = Rot([sb(es, f"w2s{i}", [128, 16, 512], BF16) for i in range(2)])
            y4 = sb(es, "y4", [128, 4, D], F32)
            y4k = [Tk() for _ in range(4)]
            rrot = Rot([sb(es, f"rr{i}", [128, 512], BF16) for i in range(4)])
            w1 = W1B
            w2 = W2B
            prot = Rot(ps[0:6])

            def load_w1(fs):
                wb = w1rot.next()
                S.dma("pool", wb[:], w1[:, fs * 256:(fs + 1) * 256].rearrange("(kt p) n -> p kt n", p=128), writes=[wb.tk])
                return wb

            def load_w2(i):
                c4, qd = i // 4, i % 4
                wb = w2rot.next()
                S.dma("pool", wb[:], w2[qd * 2048:(qd + 1) * 2048, c4 * 512:(c4 + 1) * 512].rearrange("(ft p) n -> p ft n", p=128),
                      writes=[wb.tk])
                return wb

            pend_ln = None
            for tb in range(4):
                t0 = tb * 512
                hbuf = hbufs.next()
                S.dma("sp", hbuf[:], hT_src[:, t0:t0 + 512].rearrange("(kt p) q -> p kt q", p=128), writes=[hbuf.tk])
                nw = load_w1(0)
                fins = {}
                for fs in range(32):
                    wb = nw
                    if fs + 1 < 32:
                        nw = load_w1(fs + 1)
                    else:
                        nw2 = load_w2(0)
                    for ft in range(2):
                        f = fs * 2 + ft
                        pb = prot.next()
                        for kt in range(16):
                            mm(pb[:, :], wb[:, kt, ft * 128:(ft + 1) * 128], hbuf[:, kt, :], kt == 0, kt == 15,
                               [wb.tk, hbuf.tk], [pb.tk])
                        rr = rrot.next()
                        S.op("dve", lambda e: e.tensor_scalar(rr[:, :], pb[:, :], 0.0, None, op0=ALU.max),
                             reads=[pb.tk], writes=[rr.tk])
                        S.op("dve", lambda e: e.tensor_tensor(HT[:, f, :], rr[:, :], rr[:, :], ALU.mult),
                             reads=[rr.tk], writes=[HTk[f]])
                    if pend_ln is not None:
                        if fs % 7 == 2 and fs // 7 < 4:
                            fins[fs // 7] = pend_ln(fs // 7)
                        if fs % 7 == 6 and fs // 7 < 4:
                            fins[fs // 7]()
                pend_ln = None
                for tt in range(4):
                    S.dma("sp", y4[:, tt, :], hin_d[t0 + tt * 128:t0 + (tt + 1) * 128, :], writes=[y4k[tt]])
                for i in range(16):
                    c4, qd = i // 4, i % 4
                    wb = nw2
                    if i + 1 < 16:
                        nw2 = load_w2(i + 1)
                    pa = ps[0:4] if c4 % 2 == 0 else ps[4:8]
                    for tt in range(4):
                        for ft in range(16):
                            f = qd * 16 + ft
                            mm(pa[tt][:, :], HT[:, f, tt * 128:(tt + 1) * 128], wb[:, ft, :], qd == 0 and ft == 0,
                               qd == 3 and ft == 15, [HTk[f], wb.tk], [pa[tt].tk])
                    if qd == 3:
                        for tt in range(4):
                            S.op("dve", lambda e: e.scalar_tensor_tensor(
                                out=y4[:, tt, c4 * 512:(c4 + 1) * 512], in0=y4[:, tt, c4 * 512:(c4 + 1) * 512],
                                scalar=float(ALPHA), in1=pa[tt][:, :], op0=ALU.mult, op1=ALU.add),
                                reads=[y4k[tt], pa[tt].tk], writes=[y4k[tt]])
                def mk_ln(tb=tb, hbuf=hbuf):
                    def f(tt):
                        L.cur_hTt = hbuf
                        return ln_core(L, tb * 4 + tt, y4[:, tt, :], y4k[tt], y4[:, tt, :], y4k[tt], hout_d, hTout_d)
                    return f
                pend_ln = mk_ln()
            for tt in range(4):
                pend_ln(tt)()
            S.barrier()

    def finish():
        S.barrier()
        es_glob.close()
        return nc

    order = []
    order.append(("ln_in", phase_ln_in))
    order.append(("memT", phase_memT))
    for l in range(depth):
        last = (l == depth - 1)
        a = l % 2
        b = 1 - a
        order.append((f"proj{l}", lambda l=l, a=a: phase_proj(l, hTd[a])))
        order.append((f"mla{l}", lambda l=l: phase_mla(l)))
        order.append((f"nsa{l}", lambda l=l: phase_nsa(l)))
        order.append((f"moba{l}", lambda l=l: phase_moba(l)))
        order.append((f"out{l}", lambda l=l, a=a, b=b: phase_proj_ln(CATT, WOUTB, hres[a], W["ln1_g"][l],
                                                                    W["ln1_b"][l], hres[b], hTd[b])))
        order.append((f"cross{l}", lambda l=l, b=b: phase_cross(l, hTd[b])))
        order.append((f"wo{l}", lambda l=l, a=a, b=b: phase_proj_ln(XOT, WOB, hres[b], W["ln2_g"][l],
                                                                   W["ln2_b"][l], hres[a], hTd[a])))
        if last:
            order.append((f"mlp{l}", lambda l=l, a=a: phase_mlp(l, hTd[a], hres[a], W["ln3_g"][l], W["ln3_b"][l],
                                                               out_d, None)))
        else:
            order.append((f"mlp{l}", lambda l=l, a=a, b=b: phase_mlp(l, hTd[a], hres[a], W["ln3_g"][l], W["ln3_b"][l],
                                                                    hres[b], hTd[b])))
    skip = set()
    if isinstance(stop_after, (tuple, list)):
        skip = set(stop_after[1])
        stop_after = stop_after[0]
    for name, fn in order:
        if name not in skip:
            fn()
        PHASE_LOG.append((name, S.pcnt["pe"] + 30000 * (S.nsem_pe_rot if hasattr(S, "nsem_pe_rot") else 0), S.ninst))
        if name == stop_after:
            break
    return finish()


_CACHE = {}


def kernel(**inputs):
    if "nc" not in _CACHE:
        _CACHE["nc"] = build_program()
        _CACHE["consts"] = make_consts()
    nc = _CACHE["nc"]
    consts = _CACHE["consts"]
    x = np.ascontiguousarray(np.asarray(inputs["x"], dtype=np.float32))
    mem = np.ascontiguousarray(np.asarray(inputs["mem"], dtype=np.float32))
    shared = {n: np.ascontiguousarray(np.asarray(inputs[n], dtype=np.float32)) for n, _ in W_SPECS}
    shared.update(consts)
    in_maps = []
    for b in range(8):
        m = dict(shared)
        m["x"] = x[b]
        m["mem"] = mem[b]
        in_maps.append(m)
    res = run_bass_kernel_spmd(nc, in_maps, core_ids=list(range(8)))
    return np.stack([np.asarray(r["out"], dtype=np.float32) for r in res.results], axis=0)
```
